# Optimizing a Trainium2 kernel written in Bass

```python
import jax, jax.numpy as jnp
from jax import lax
import numpy as np

D_MODEL = 4096
BATCH = 2
SEQ = 8192
DEPTH = 1

CHUNK = 64
Q_BLOCK = 128
RET_HEADS = 8
RET_HEAD_DIM = 256
RET_WIDTH = RET_HEADS * RET_HEAD_DIM
FOX_HEADS = 16
FOX_HEAD_DIM = 128
FOX_WIDTH = FOX_HEADS * FOX_HEAD_DIM
N_BRANCH = 2
ROPE_BASE = 10000.0
LN_EPS = 1e-5
DEEPNORM_ALPHA = (2.0 * DEPTH) ** 0.25
DEEPNORM_BETA = (8.0 * DEPTH) ** -0.25
COL_SIZES = (RET_WIDTH, RET_WIDTH, RET_WIDTH, RET_WIDTH,
             FOX_WIDTH, FOX_WIDTH, FOX_WIDTH, FOX_WIDTH,
             FOX_HEADS, N_BRANCH * D_MODEL)
IN_COLS = 4 * RET_WIDTH + 4 * FOX_WIDTH + FOX_HEADS + N_BRANCH * D_MODEL

kernel_name = "hybrid_retention_fox_gated_deepnorm"


def layer_norm(x, g, b):
    xf = x.astype(jnp.float32)
    mu = jnp.mean(xf, axis=-1, keepdims=True)
    var = jnp.mean(jnp.square(xf - mu), axis=-1, keepdims=True)
    y = (xf - mu) * lax.rsqrt(var + LN_EPS) * g.astype(jnp.float32) + b.astype(jnp.float32)
    return y.astype(x.dtype)


def rotary(x, pos):
    half = x.shape[-1] // 2
    inv_freq = ROPE_BASE ** (-jnp.arange(half, dtype=jnp.float32) / half)
    ang = pos[:, None] * inv_freq[None, :]
    cos = jnp.cos(ang)[None, :, None, :]
    sin = jnp.sin(ang)[None, :, None, :]
    x1 = x[..., :half].astype(jnp.float32)
    x2 = x[..., half:].astype(jnp.float32)
    return jnp.concatenate([x1 * cos - x2 * sin, x1 * sin + x2 * cos], axis=-1)


def retention_chunkwise(q, k, v):
    B, S, H, dk = q.shape
    dv = v.shape[-1]
    n = S // CHUNK
    log_gamma = jnp.log1p(-(2.0 ** (-5.0 - jnp.arange(H, dtype=jnp.float32))))
    idx = jnp.arange(CHUNK, dtype=jnp.float32)
    intra = jnp.exp(log_gamma[:, None, None] * jnp.abs(idx[:, None] - idx[None, :]))
    q_dec = jnp.exp(log_gamma[:, None] * idx[None, :])[None, :, :, None]
    k_dec = jnp.exp(log_gamma[:, None] * (CHUNK - idx)[None, :])[None, :, :, None]
    chunk_dec = jnp.exp(log_gamma * CHUNK)[None, :, None, None]

    def to_chunks(t):
        return t.astype(jnp.float32).reshape(B, n, CHUNK, H, -1).transpose(1, 0, 3, 2, 4)

    qc = to_chunks(q)
    kc = to_chunks(k) * (dk ** -0.5)
    vc = to_chunks(v)

    def step(state, inp):
        qi, ki, vi = inp
        scores = jnp.einsum('bhjd,bhld->bhjl', qi, ki) * intra[None]
        inner = jnp.einsum('bhjl,bhle->bhje', scores, vi)
        cross = jnp.einsum('bhjd,bhde->bhje', qi * q_dec, state)
        state = state * chunk_dec + jnp.einsum('bhld,bhle->bhde', ki * k_dec, vi)
        return state, inner + cross

    s0 = jnp.zeros((B, H, dk, dv), jnp.float32)
    _, out = lax.scan(step, s0, (qc, kc, vc))
    return out.transpose(1, 0, 3, 2, 4).reshape(B, S, H, dv)


def head_group_norm(y, g, b):
    yf = y.astype(jnp.float32)
    mu = jnp.mean(yf, axis=-1, keepdims=True)
    var = jnp.mean(jnp.square(yf - mu), axis=-1, keepdims=True)
    yn = (yf - mu) * lax.rsqrt(var + LN_EPS)
    B, S = y.shape[0], y.shape[1]
    return yn.reshape(B, S, -1) * g.astype(jnp.float32) + b.astype(jnp.float32)


def forgetting_attention(q, k, v, f_logit):
    B, S, H, d = q.shape
    cum = jnp.cumsum(jax.nn.log_sigmoid(f_logit.astype(jnp.float32)), axis=1).transpose(0, 2, 1)
    qh = q.transpose(0, 2, 1, 3)
    kh = k.transpose(0, 2, 1, 3)
    vh = v.transpose(0, 2, 1, 3)
    scale = d ** -0.5
    outs = []
    for start in range(0, S, Q_BLOCK):
        end = start + Q_BLOCK
        qb = qh[:, :, start:end]
        kb = kh[:, :, :end]
        vb = vh[:, :, :end]
        logits = (jnp.einsum('bhqd,bhkd->bhqk', qb, kb).astype(jnp.float32) * scale
                  + cum[:, :, start:end, None] - cum[:, :, None, :end])
        t_idx = jnp.arange(start, end)
        s_idx = jnp.arange(end)
        logits = jnp.where(s_idx[None, :] <= t_idx[:, None], logits, -jnp.inf)
        p = jax.nn.softmax(logits, axis=-1)
        outs.append(jnp.einsum('bhqk,bhkd->bhqd', p.astype(vb.dtype), vb))
    out = jnp.concatenate(outs, axis=2)
    return out.transpose(0, 2, 1, 3)


def setup_inputs(seed: int = 0) -> dict:
    key = jax.random.key(seed)
    ks = jax.random.split(key, 14)
    f32 = jnp.float32
    x = jax.random.normal(ks[0], (BATCH, SEQ, D_MODEL), f32)
    ln_in_g = 1.0 + 0.02 * jax.random.normal(ks[1], (D_MODEL,), f32)
    ln_in_b = 0.02 * jax.random.normal(ks[2], (D_MODEL,), f32)
    col_scale = np.ones((IN_COLS,), np.float32)
    col_scale[2 * RET_WIDTH:3 * RET_WIDTH] = DEEPNORM_BETA
    v_f0 = 4 * RET_WIDTH + 2 * FOX_WIDTH
    col_scale[v_f0:v_f0 + FOX_WIDTH] = DEEPNORM_BETA
    w_in = (jax.random.normal(ks[3], (DEPTH, D_MODEL, IN_COLS), f32)
            * (D_MODEL ** -0.5) * jnp.asarray(col_scale)[None, None, :])
    b_gate = 0.02 * jax.random.normal(ks[4], (DEPTH, N_BRANCH * D_MODEL), f32)
    b_forget = jax.random.uniform(ks[5], (DEPTH, FOX_HEADS), f32, minval=1.0, maxval=6.0)
    ret_norm_g = 1.0 + 0.02 * jax.random.normal(ks[6], (DEPTH, RET_WIDTH), f32)
    ret_norm_b = 0.02 * jax.random.normal(ks[7], (DEPTH, RET_WIDTH), f32)
    w_branch_ret = jax.random.normal(ks[8], (DEPTH, RET_WIDTH, D_MODEL), f32) * (RET_WIDTH ** -0.5) * DEEPNORM_BETA
    w_branch_fox = jax.random.normal(ks[9], (DEPTH, FOX_WIDTH, D_MODEL), f32) * (FOX_WIDTH ** -0.5) * DEEPNORM_BETA
    w_out = jax.random.normal(ks[10], (DEPTH, D_MODEL, D_MODEL), f32) * (D_MODEL ** -0.5) * DEEPNORM_BETA
    ln_post_g = 1.0 + 0.02 * jax.random.normal(ks[11], (DEPTH, D_MODEL), f32)
    ln_post_b = 0.02 * jax.random.normal(ks[12], (DEPTH, D_MODEL), f32)
    return {"x": x, "ln_in_g": ln_in_g, "ln_in_b": ln_in_b, "w_in": w_in, "b_gate": b_gate,
            "b_forget": b_forget, "ret_norm_g": ret_norm_g, "ret_norm_b": ret_norm_b,
            "w_branch_ret": w_branch_ret, "w_branch_fox": w_branch_fox, "w_out": w_out,
            "ln_post_g": ln_post_g, "ln_post_b": ln_post_b}


def reference(x, ln_in_g, ln_in_b, w_in, b_gate, b_forget, ret_norm_g, ret_norm_b,
              w_branch_ret, w_branch_fox, w_out, ln_post_g, ln_post_b):
    B, S, D = x.shape
    split_at = []
    acc = 0
    for sz in COL_SIZES[:-1]:
        acc += sz
        split_at.append(acc)
    pos = jnp.arange(S, dtype=jnp.float32)
    h = layer_norm(x, ln_in_g, ln_in_b)
    for l in range(DEPTH):
        proj = jnp.einsum('bsd,dn->bsn', h, w_in[l])
        (q_r, k_r, v_r, g_r, q_f, k_f, v_f, g_f,
         f_logit, gate_logit) = jnp.split(proj, split_at, axis=-1)

        q_r = rotary(q_r.reshape(B, S, RET_HEADS, RET_HEAD_DIM), pos)
        k_r = rotary(k_r.reshape(B, S, RET_HEADS, RET_HEAD_DIM), pos)
        v_r = v_r.reshape(B, S, RET_HEADS, RET_HEAD_DIM)
        y_r = head_group_norm(retention_chunkwise(q_r, k_r, v_r), ret_norm_g[l], ret_norm_b[l])
        y_r = (y_r * jax.nn.silu(g_r.astype(jnp.float32))).astype(h.dtype)
        p_r = jnp.einsum('bsc,cd->bsd', y_r, w_branch_ret[l])

        q_f = q_f.reshape(B, S, FOX_HEADS, FOX_HEAD_DIM)
        k_f = k_f.reshape(B, S, FOX_HEADS, FOX_HEAD_DIM)
        v_f = v_f.reshape(B, S, FOX_HEADS, FOX_HEAD_DIM)
        y_f = forgetting_attention(q_f, k_f, v_f, f_logit + b_forget[l]).reshape(B, S, FOX_WIDTH)
        y_f = (y_f.astype(jnp.float32) * jax.nn.silu(g_f.astype(jnp.float32))).astype(h.dtype)
        p_f = jnp.einsum('bsc,cd->bsd', y_f, w_branch_fox[l])

        gates = jax.nn.sigmoid((gate_logit + b_gate[l]).astype(jnp.float32)).reshape(B, S, N_BRANCH, D)
        merged = (gates[:, :, 0] * p_r.astype(jnp.float32)
                  + gates[:, :, 1] * p_f.astype(jnp.float32)).astype(h.dtype)
        y = jnp.einsum('bsd,de->bse', merged, w_out[l])

        h = layer_norm(DEEPNORM_ALPHA * h + y, ln_post_g[l], ln_post_b[l])
    return h
```

```python
import numpy as np
import concourse.bass as bass
import concourse.mybir as mybir

F32 = mybir.dt.float32
BF16 = mybir.dt.bfloat16
U8 = mybir.dt.uint8
AF = mybir.ActivationFunctionType
ALU = mybir.AluOpType
AX = mybir.AxisListType

ENG_NAMES = ("pe", "act", "dve", "pool", "sp")


class Buf:
    __slots__ = ("name", "w", "r")

    def __init__(self, name=""):
        self.name = name
        self.w = {}
        self.r = {}


def _merge(dst, src):
    for k, v in src.items():
        if dst.get(k, -1) < v:
            dst[k] = v


class Slot:
    __slots__ = ("sid", "count")

    def __init__(self, sid):
        self.sid = sid
        self.count = 0


class Prog:
    def __init__(self, nc):
        self.nc = nc
        self.q = {e: [] for e in ENG_NAMES}
        self.nslots = 0
        self._slots = []
        self.free = []
        self.pre = {e: [] for e in ENG_NAMES}
        self.env = {}

    def slot(self):
        if self.free:
            return self.free.pop()
        s = Slot(self.nslots)
        self.nslots += 1
        self._slots.append(s)
        return s

    def _deps(self, eng, reads, writes):
        deps = {}
        for b in reads:
            _merge(deps, b.w)
        for b in writes:
            _merge(deps, b.w)
            _merge(deps, b.r)
        if eng == "pe":
            deps.pop(("c", "pe"), None)
        return deps

    def op(self, eng, fn, reads=(), writes=()):
        idx = len(self.q[eng])
        deps = self._deps(eng, reads, writes)
        self.q[eng].append([fn, deps, False, None])
        key = ("c", eng)
        for b in reads:
            if b.r.get(key, -1) < idx:
                b.r[key] = idx
        for b in writes:
            b.w = {key: idx}
            b.r = {}
        return (eng, idx)

    def dma(self, eng, out, in_, slot, reads=(), writes=(), **kw):
        deps = self._deps(eng, reads, writes)
        slot.count += 16
        val = slot.count

        def fn(e, out=out, in_=in_, kw=kw):
            return e.dma_start(out=out, in_=in_, **kw)

        self.q[eng].append([fn, deps, False, slot.sid])
        key = ("d", slot.sid)
        for b in reads:
            if b.r.get(key, -1) < val:
                b.r[key] = val
        for b in writes:
            b.w = {key: val}
            b.r = {}

    def custom(self, eng, fn, slot, inc, reads=(), writes=()):
        deps = self._deps(eng, reads, writes)
        slot.count += inc
        val = slot.count
        self.q[eng].append([fn, deps, False, (slot.sid, inc)])
        key = ("d", slot.sid)
        for b in reads:
            if b.r.get(key, -1) < val:
                b.r[key] = val
        for b in writes:
            b.w = {key: val}
            b.r = {}

    def wait_all(self, eng, bufs):
        deps = {}
        for b in bufs:
            _merge(deps, b.w)
            _merge(deps, b.r)
        self.q[eng].append([None, deps, False, None])

    def last(self, buf, *engs):
        buf.w = {("c", e): len(self.q[e]) - 1 for e in engs}
        buf.r = {}

    def n_instr(self):
        return {e: len(v) for e, v in self.q.items()}

    def replay(self, stack):
        nc = self.nc
        for e in ENG_NAMES:
            for rec in self.q[e]:
                for k, v in rec[1].items():
                    if k[0] == "c":
                        self.q[k[1]][v][2] = True
        prefix = {}
        for e in ENG_NAMES:
            c = 0
            pl = []
            for rec in self.q[e]:
                if rec[2]:
                    c += 1
                pl.append(c)
            prefix[e] = pl
        csem = {e: stack.enter_context(nc.semaphore("c_" + e)) for e in ENG_NAMES if e != "sp"}
        dsem = [stack.enter_context(nc.semaphore("d_%d" % i)) for i in range(self.nslots)]
        block = stack.enter_context(nc.Block())
        q = self.q

        def run(ename, eng):
            waited = {}
            for f in self.pre[ename]:
                f(eng)
            for rec in q[ename]:
                fn, deps, sig, dm = rec
                for k, v in deps.items():
                    if k[0] == "c":
                        sem = csem[k[1]]
                        val = prefix[k[1]][v]
                        wk = k[1]
                    else:
                        sem = dsem[k[1]]
                        val = v
                        wk = k
                    if waited.get(wk, -1) >= val:
                        continue
                    waited[wk] = val
                    eng.wait_ge(sem, val)
                if fn is None:
                    continue
                ins = fn(eng)
                if dm is not None:
                    if isinstance(dm, tuple):
                        ins.then_inc(dsem[dm[0]], dm[1])
                    else:
                        ins.then_inc(dsem[dm], 16)
                elif sig:
                    ins.then_inc(csem[ename], 1)

        @block.tensor
        def _(e):
            run("pe", e)

        @block.scalar
        def _(e):
            run("act", e)

        @block.vector
        def _(e):
            run("dve", e)

        @block.gpsimd
        def _(e):
            run("pool", e)

        @block.sync
        def _(e):
            run("sp", e)


class Arena:
    def __init__(self, nc, stack, nbytes, name="arena"):
        self.t = stack.enter_context(nc.sbuf_tensor(name, [128, nbytes], U8))
        self.nbytes = nbytes
        self.off = 0

    def alloc(self, cols, dtype, parts=128):
        esz = 4 if dtype == F32 else 2
        nb = cols * esz
        a = (self.off + 63) // 64 * 64
        assert a + nb <= self.nbytes, ("arena overflow", a + nb, self.nbytes)
        self.off = a + nb
        v = self.t[0:parts, a:a + nb].bitcast(dtype)
        return v

    def mark(self):
        return self.off

    def reset(self, m):
        self.off = m

from contextlib import ExitStack

RET_HD = 256
FOX_HD = 128
LN_EPS = 1e-5
I32 = mybir.dt.int32


class Cfg:
    def __init__(self, D=4096, S=8192, TS=2048, depth=1):
        self.D, self.S, self.TS = D, S, TS
        self.KC = D // 128
        self.TB = S // 4
        self.NBK = S // 128
        self.alpha = (2.0 * depth) ** 0.25


def barrier(p):
    deps = {}
    for e in ENG_NAMES:
        if e == "sp":
            continue
        for i in range(len(p.q[e]) - 1, -1, -1):
            if p.q[e][i][0] is not None and p.q[e][i][3] is None:
                deps[("c", e)] = i
                break
    pinned = getattr(p, "pinned", set())
    for s in p._slots:
        if s.count and s.sid not in pinned:
            deps[("d", s.sid)] = s.count
    for e in ENG_NAMES:
        d = dict(deps)
        if e == "pe":
            d.pop(("c", "pe"), None)
        p.q[e].append([None, d, False, None])
    p.free = [s for s in p._slots if s.sid not in pinned]


def build_program(cfg, dbg=False, upto=99):
    D, S, TS, KC, TB, NBK = cfg.D, cfg.S, cfg.TS, cfg.KC, cfg.TB, cfg.NBK
    NCG = 2 * D // 128
    NCD = D // 128
    NGP = 2
    nc = bass.Bass("TRN2", target_bir_lowering=False)

    def din(name, shape, dt=F32):
        return nc.dram_tensor(name, list(shape), dt, kind="ExternalInput").ap()

    def dscr(name, shape, dt=F32, force_internal=False):
        kind = "ExternalOutput" if (dbg and not force_internal) else "Internal"
        return nc.dram_tensor(name, list(shape), dt, kind=kind).ap()

    x = din("x", [S, D])
    xoff = din("xoff_rows", [TB, D])
    tokoff_d = din("tokoff", [1, 1], I32)
    wa = din("wa", [32, 128, KC * 128])
    wf = din("wf", [128, KC * 4])
    wg = din("wg", [NCG, 128, KC * 128])
    wbr = din("wbr", [NCD, 128, 16 * 128])
    wbf = din("wbf", [NCD, 128, 16 * 128])
    wo = din("wo", [NCD, 128, KC * 128])
    NV = 2 * KC + NCG + 2 + 1 + 8
    cvec_d = din("cvec", [128, NV])
    rows_d = din("rows", [4 * D + 1024])
    cos_d = din("cos", [128, S])
    sin_d = din("sin", [128, S])
    identb_d = din("ident_bf", [128, 128], BF16)
    identf_d = din("ident_f", [128, 128])
    triu_d = din("triu", [128, 128])
    stri_d = din("stri", [NBK, NBK])
    mask_d = din("maskT", [4, 128, 512], BF16)
    intra_d = din("intra", [2, 64, 64])
    qdec_d = din("qdec", [2, 128, 512])
    kdec_d = din("kdec", [2, 128, 512])

    s_qk = dscr("s_qk", [2, 4, 128, S])
    s_vr = dscr("s_vr", [2, 2, 128, S], BF16)
    s_sgr = dscr("s_sgr", [2, 2, 128, S])
    s_qf = dscr("s_qf", [4, 128, S], BF16)
    s_kf = dscr("s_kf", [4, 128, S], BF16)
    s_vf = dscr("s_vf", [4, 128, S], BF16)
    s_sgf = dscr("s_sgf", [4, 128, S])
    s_lf = dscr("s_lf", [4, S])
    s_crow = dscr("s_crow", [4, 3, S], BF16)
    cin = dscr("cin", [4, 1024, TB], BF16, force_internal=True)
    cout = dscr("cout", [4, 4096, TB], BF16, force_internal=True)
    cin_dbg = dscr("cin_dbg", [4, 1024, TB], BF16) if dbg else None

    def cin_ap(rb, t0):
        cq, off = divmod(t0, TB)
        return cin[cq, rb * 128:(rb + 1) * 128, off:off + 512]
    s_gates = dscr("s_gates", [NCG, 128, TB])
    s_merged = dscr("s_merged", [NCD, 128, TB], BF16)
    s_yo = dscr("s_yo", [TB, D])
    out = nc.dram_tensor("out", [TB, D], F32, kind="ExternalOutput").ap()
    OUTB = Buf("out")

    st = ExitStack()
    p = Prog(nc)
    ar = Arena(nc, st, 206 * 1024)
    banks = [st.enter_context(nc.psum_tensor("bank%d" % i, [128, 512], F32)) for i in range(8)]

    ident_bf = ar.alloc(128, BF16)
    ident_f = ar.alloc(128, F32)
    ones_bf = ar.alloc(128, BF16)
    ones_f = ar.alloc(128, F32)
    cvec = ar.alloc(NV, F32)
    negbf = ar.alloc(1, F32)
    eps_t = ar.alloc(1, F32)
    one_t = ar.alloc(1, F32)
    cB = Buf("consts")
    s_c = p.slot()
    p.dma("sp", ident_bf, identb_d, s_c, writes=[cB])
    p.dma("sp", ident_f, identf_d, s_c, writes=[cB])
    p.dma("sp", cvec, cvec_d, s_c, writes=[cB])
    p.op("dve", lambda e: e.memset(ones_bf, 1.0))
    p.op("dve", lambda e: e.memset(ones_f, 1.0))
    p.op("dve", lambda e: e.memset(eps_t, LN_EPS))
    p.op("dve", lambda e: e.memset(one_t, 1.0))
    CV_G, CV_B, CV_BG, CV_CD, CV_BF = 0, KC, 2 * KC, 2 * KC + NCG, 2 * KC + NCG + 2
    CV_GN = CV_BF + 1
    p.op("dve", lambda e: e.tensor_scalar(out=negbf[0:4, :], in0=cvec[0:4, CV_BF:CV_BF + 1], scalar1=-1.0,
                                          scalar2=None, op0=ALU.mult), reads=[cB])
    ACT_BYTES = max(32 * max(TS, TB) * 2, 112 * 1024)
    actT_raw = ar.alloc(ACT_BYTES // 2, BF16)
    base_mark = ar.mark()
    barrier(p)

    def actT_view(nchunk, T):
        return actT_raw[:, 0:nchunk * T].rearrange("p (k t) -> p k t", k=nchunk)

    class BigAlloc:
        def __init__(self):
            self.off = 0

        def __call__(self, cols, dt, parts=128):
            esz = 2 if dt == BF16 else 4
            n16 = cols * esz // 2
            a = self.off
            self.off += (n16 + 31) // 32 * 32
            assert self.off * 2 <= ACT_BYTES, ("actT region overflow", self.off * 2, ACT_BYTES)
            v = actT_raw[0:parts, a:a + n16]
            return v if dt == BF16 else v.bitcast(F32)

    def stage_lnt(xsrc, n_tt, T):
        ar.reset(base_mark)
        aT = actT_view(KC, T)
        xb = [ar.alloc(D, F32) for _ in range(4)]
        nch = max(D // 512, 1)
        cw = D // nch
        stt = [ar.alloc(nch * 6, F32) for _ in range(4)]
        mv = [ar.alloc(2, F32) for _ in range(4)]
        sd = [ar.alloc(1, F32) for _ in range(4)]
        rs = [ar.alloc(1, F32) for _ in range(4)]
        XB = [Buf() for _ in range(4)]
        SM = [Buf() for _ in range(4)]
        PSB = [Buf() for _ in range(4)]
        sl = [p.slot() for _ in range(4)]

        def load(tt):
            k = tt % 4
            p.dma("sp", xb[k], xsrc[tt * 128:(tt + 1) * 128, :], sl[k], writes=[XB[k]])

        load(0)
        if n_tt > 1:
            load(1)
        for tt in range(n_tt):
            k = tt % 4
            if tt + 2 < n_tt:
                load(tt + 2)
            for i in range(nch):
                p.op("dve", lambda e, k=k, i=i: e.bn_stats(out=stt[k][:, i * 6:(i + 1) * 6], in_=xb[k][:, i * cw:(i + 1) * cw]),
                     reads=[XB[k]], writes=[SM[k]] if i == 0 else [])
            p.op("dve", lambda e, k=k: e.bn_aggr(out=mv[k], in_=stt[k]), reads=[SM[k]], writes=[SM[k]])
            p.op("act", lambda e, k=k: e.activation(out=sd[k], in_=mv[k][:, 1:2], func=AF.Sqrt, bias=eps_t[:, 0:1], scale=1.0),
                 reads=[SM[k]], writes=[SM[k]])
            p.op("dve", lambda e, k=k: e.reciprocal(out=rs[k], in_=sd[k]), reads=[SM[k]], writes=[SM[k]])
            p.op("dve", lambda e, k=k: e.tensor_scalar(out=sd[k], in0=mv[k][:, 0:1], scalar1=rs[k][:, 0:1], scalar2=-1.0,
                                                       op0=ALU.mult, op1=ALU.mult), reads=[SM[k]], writes=[SM[k]])
            p.op("act", lambda e, k=k: e.activation(out=xb[k], in_=xb[k], func=AF.Identity, scale=rs[k][:, 0:1], bias=sd[k][:, 0:1]),
                 reads=[XB[k], SM[k]], writes=[XB[k]])
            pair, j = divmod(tt, 2)
            if j == 1 or tt == n_tt - 1:
                nj = j + 1
                ks = [(tt - j + jj) % 4 for jj in range(nj)]
                for kc in range(KC):
                    bk = kc % 4
                    pv = banks[bk][:, 0:128 * nj]
                    for jj in range(nj):
                        p.op("pe", lambda e, kk=ks[jj], jj=jj, kc=kc, pv=pv: e.transpose(out=pv[:, jj * 128:(jj + 1) * 128],
                                                                                      in_=xb[kk][:, kc * 128:(kc + 1) * 128],
                                                                                      identity=ident_f),
                             reads=[XB[ks[jj]]], writes=[PSB[bk]] if jj == 0 else [])
                    p.last(PSB[bk], "pe")
                    dst = aT[:, kc, pair * 256: pair * 256 + 128 * nj]
                    if kc % 2 == 0:
                        p.op("act", lambda e, pv=pv, dst=dst, kc=kc: e.activation(out=dst, in_=pv, func=AF.Identity,
                                                                                 scale=cvec[:, CV_G + kc:CV_G + kc + 1],
                                                                                 bias=cvec[:, CV_B + kc:CV_B + kc + 1]),
                             reads=[PSB[bk]])
                    else:
                        p.op("dve", lambda e, pv=pv, dst=dst, kc=kc: e.tensor_scalar(out=dst, in0=pv, scalar1=cvec[:, CV_G + kc:CV_G + kc + 1],
                                                                                    scalar2=cvec[:, CV_B + kc:CV_B + kc + 1],
                                                                                    op0=ALU.mult, op1=ALU.add),
                             reads=[PSB[bk]])
        barrier(p)

    def stage_proj(blocks, nchunk, T, evac, work_mark, pre=None):
        aT = actT_view(nchunk, T)
        NG = T // 512
        ngp = min(NGP, NG)
        npass = NG // ngp
        ar.reset(work_mark)
        wsz = max(nk * M for (_, M, nk, _) in blocks)
        wf32 = [ar.alloc(wsz, F32) for _ in range(2)]
        wb16 = [ar.alloc(wsz, BF16) for _ in range(2)]
        WF = [Buf() for _ in range(2)]
        WB = [Buf() for _ in range(2)]
        WB2 = [Buf() for _ in range(2)]
        PS = [[Buf() for _ in range(ngp)] for _ in range(2)]
        sl = [p.slot() for _ in range(2)]

        def load(i):
            w_ap, M, nk, _ = blocks[i]
            k = i % 2
            p.dma("sp", wf32[k][:, 0:nk * M], w_ap, sl[k], writes=[WF[k]])
            half = (nk * M) // 2
            p.op("dve", lambda e, k=k, half=half: e.tensor_copy(out=wb16[k][:, 0:half], in_=wf32[k][:, 0:half]),
                 reads=[WF[k]], writes=[WB[k]])
            p.op("act", lambda e, k=k, half=half, n=nk * M: e.copy(out=wb16[k][:, half:n], in_=wf32[k][:, half:n]),
                 reads=[WF[k]], writes=[WB2[k]])

        load(0)
        u = 0
        for i, (w_ap, M, nk, kmap) in enumerate(blocks):
            k = i % 2
            if i + 1 < len(blocks):
                load(i + 1)
            if pre is not None:
                pre(i)
            for ps in range(npass):
                sset = u % 2
                u += 1
                for kc in range(nk):
                    for t in range(ngp):
                        tg = ps * ngp + t
                        bank = banks[sset * ngp + t]
                        p.op("pe", lambda e, bank=bank, k=k, kc=kc, tg=tg, M=M, nk=nk, kmap=kmap: e.matmul(
                            bank[0:M, :], lhsT=wb16[k][:, kc * M:(kc + 1) * M], rhs=aT[:, kmap[kc], tg * 512:(tg + 1) * 512],
                            start=(kc == 0), stop=(kc == nk - 1)),
                            reads=[WB[k], WB2[k]], writes=[PS[sset][t]] if kc == 0 else [])
                for t in range(ngp):
                    p.last(PS[sset][t], "pe")
                    evac(i, ps * ngp + t, banks[sset * ngp + t], PS[sset][t])
        barrier(p)

    A_blocks = [(wa[cb], 128, KC, list(range(KC))) for cb in range(32)]
    A_blocks.append((wf, 4, KC, list(range(KC))))

    def phaseA_proj(t0):
        ar.reset(base_mark)
        of32 = [ar.alloc(512, F32) for _ in range(3)]
        ob16 = [ar.alloc(512, BF16) for _ in range(3)]
        tmpf = [ar.alloc(512, F32) for _ in range(2)]
        OF = [Buf() for _ in range(3)]
        OB = [Buf() for _ in range(3)]
        TF = [Buf() for _ in range(2)]
        osl = [p.slot() for _ in range(8)]
        wm2 = ar.mark()
        cnt = [0, 0, 0]

        def evacA(i, tg, bank, PSb):
            tok = slice(t0 + tg * 512, t0 + (tg + 1) * 512)
            if i < 16:
                hl, typ = divmod(i, 8)
                if typ < 4:
                    k = cnt[0] % 3
                    cnt[0] += 1
                    if cnt[0] % 2:
                        p.op("dve", lambda e, k=k, bank=bank: e.tensor_copy(out=of32[k], in_=bank[:, :]), reads=[PSb], writes=[OF[k]])
                    else:
                        p.op("act", lambda e, k=k, bank=bank: e.copy(out=of32[k], in_=bank[:, :]), reads=[PSb], writes=[OF[k]])
                    p.dma("sp", s_qk[hl, typ, :, tok], of32[k], osl[k], reads=[OF[k]])
                elif typ < 6:
                    k = cnt[1] % 3
                    cnt[1] += 1
                    p.op("dve", lambda e, k=k, bank=bank: e.tensor_copy(out=ob16[k], in_=bank[:, :]), reads=[PSb], writes=[OB[k]])
                    p.dma("sp", s_vr[hl, typ - 4, :, tok], ob16[k], osl[3 + k], reads=[OB[k]])
                else:
                    k = cnt[0] % 3
                    cnt[0] += 1
                    p.op("act", lambda e, k=k, bank=bank: e.activation(out=of32[k], in_=bank[:, :], func=AF.Silu), reads=[PSb], writes=[OF[k]])
                    p.dma("sp", s_sgr[hl, typ - 6, :, tok], of32[k], osl[k], reads=[OF[k]])
            elif i < 32:
                fl, typ = divmod(i - 16, 4)
                k = cnt[1] % 3
                cnt[1] += 1
                if typ == 0:
                    p.op("act", lambda e, k=k, bank=bank: e.activation(out=ob16[k], in_=bank[:, :], func=AF.Copy, scale=float(FOX_HD) ** -0.5),
                         reads=[PSb], writes=[OB[k]])
                    dst = s_qf
                elif typ == 1:
                    p.op("dve", lambda e, k=k, bank=bank: e.tensor_copy(out=ob16[k], in_=bank[:, :]), reads=[PSb], writes=[OB[k]])
                    dst = s_kf
                elif typ == 2:
                    p.op("dve", lambda e, k=k, bank=bank: e.tensor_copy(out=ob16[k], in_=bank[:, :]), reads=[PSb], writes=[OB[k]])
                    dst = s_vf
                else:
                    dst = None
                    k = cnt[0] % 3
                    cnt[0] += 1
                    p.op("act", lambda e, k=k, bank=bank: e.activation(out=of32[k], in_=bank[:, :], func=AF.Silu), reads=[PSb], writes=[OF[k]])
                    p.dma("sp", s_sgf[fl, :, tok], of32[k], osl[k], reads=[OF[k]])
                if dst is not None:
                    p.dma("sp", dst[fl, :, tok], ob16[k], osl[3 + k], reads=[OB[k]])
            else:
                k = cnt[2] % 2
                cnt[2] += 1
                p.op("act", lambda e, k=k, bank=bank: e.activation(out=tmpf[k][0:4, :], in_=bank[0:4, :], func=AF.Exp,
                                                                   scale=-1.0, bias=negbf[0:4, 0:1]),
                     reads=[PSb], writes=[TF[k]])
                p.op("act", lambda e, k=k: e.activation(out=tmpf[k][0:4, :], in_=tmpf[k][0:4, :], func=AF.Ln, bias=one_t[0:4, 0:1], scale=1.0),
                     reads=[TF[k]], writes=[TF[k]])
                p.dma("sp", s_lf[:, tok], tmpf[k][0:4, :], osl[6 + k], reads=[TF[k]])

        stage_proj(A_blocks, KC, TS, evacA, wm2)

    import os as _os
    if upto >= 1 and not _os.environ.get('SKIPA'):
        for stile in range(S // TS):
            stage_lnt(x[stile * TS:(stile + 1) * TS, :], TS // 128, TS)
            phaseA_proj(stile * TS)

    def stage_ret():
        ar.reset(base_mark)
        balloc = BigAlloc()
        NTL = S // 512
        intra = ar.alloc(2 * 64, F32, parts=64)
        qdec = [ar.alloc(512, F32) for _ in range(2)]
        kdec = [ar.alloc(512, F32) for _ in range(2)]
        gng = ar.alloc(512, F32, parts=64)
        gnb = ar.alloc(512, F32, parts=64)
        CB = Buf()
        s0 = p.slot()
        for h in range(2):
            p.dma("sp", intra[:, h * 64:(h + 1) * 64], intra_d[h], s0, writes=[CB])
            p.dma("sp", qdec[h], qdec_d[h], s0, writes=[CB])
            p.dma("sp", kdec[h], kdec_d[h], s0, writes=[CB])
        p.dma("sp", gng, rows_d[4 * D:4 * D + 512].partition_broadcast(64), s0, writes=[CB])
        p.dma("sp", gnb, rows_d[4 * D + 512:4 * D + 1024].partition_broadcast(64), s0, writes=[CB])
        raw = [[balloc(512, F32) for _ in range(4)] for _ in range(2)]
        cs = [[balloc(512, F32) for _ in range(2)] for _ in range(2)]
        vT = [[balloc(512, BF16) for _ in range(2)] for _ in range(2)]
        sg = [balloc(1024, F32) for _ in range(2)]
        t1 = [balloc(512, F32) for _ in range(4)]
        rot = [[balloc(512, BF16) for _ in range(8)] for _ in range(2)]
        kv = [balloc(512, BF16) for _ in range(2)]
        ssb = [balloc(64, BF16) for _ in range(2)]
        state = balloc(512, F32)
        stb = [balloc(512, BF16) for _ in range(2)]
        gst = [balloc(6, F32) for _ in range(2)]
        gmv = [balloc(2, F32) for _ in range(2)]
        gsd = [balloc(1, F32) for _ in range(2)]
        grs = [balloc(1, F32) for _ in range(2)]
        ynf = [balloc(256, F32) for _ in range(2)]
        ynb = [balloc(8 * 256, BF16) for _ in range(2)]
        yts = [balloc(1024, BF16) for _ in range(2)]
        yaf = [balloc(512, F32) for _ in range(2)]
        YAF = [Buf() for _ in range(2)]
        RAW = [Buf() for _ in range(2)]
        ROTQ = [Buf() for _ in range(2)]
        ROTK = [Buf() for _ in range(2)]
        T1 = [Buf() for _ in range(4)]
        KV = [Buf() for _ in range(2)]
        SSB = [Buf() for _ in range(2)]
        STATE = Buf()
        STB = [Buf() for _ in range(2)]
        GS = [Buf() for _ in range(2)]
        YNF = [Buf() for _ in range(2)]
        YNB = [Buf() for _ in range(2)]
        YTS = [Buf() for _ in range(2)]
        BT = [Buf(), Buf()]
        BSO = [Buf(), Buf()]
        BU = [Buf(), Buf()]
        BYT = [Buf(), Buf()]
        lsl = [p.slot() for _ in range(2)]
        ysl = [p.slot() for _ in range(2)]

        def load_tile(hl, tl, k):
            tok = slice(tl * 512, (tl + 1) * 512)
            for c in range(4):
                p.dma("sp", raw[k][c], s_qk[hl, c, :, tok], lsl[k], writes=[RAW[k]] if c == 0 else [])
            p.dma("sp", cs[k][0], cos_d[:, tok], lsl[k])
            p.dma("sp", cs[k][1], sin_d[:, tok], lsl[k])
            for ec in range(2):
                p.dma("sp", vT[k][ec], s_vr[hl, ec, :, tok], lsl[k])
                p.dma("sp", sg[k][:, ec * 512:(ec + 1) * 512], s_sgr[hl, ec, :, tok], lsl[k])
            RAW[k].w = {("d", lsl[k].sid): lsl[k].count}
            RAW[k].r = {}

        n = 0
        gt = 0
        for hl in range(2):
            p.op("dve", lambda e: e.memset(state, 0.0), writes=[STATE])
            p.op("dve", lambda e, m=n % 2: e.memset(stb[m], 0.0), writes=[STB[n % 2]])
            load_tile(hl, 0, gt % 2)
            for tl in range(NTL):
                k = gt % 2
                gt += 1
                if tl + 1 < NTL:
                    load_tile(hl, tl + 1, 1 - k)
                R = rot[k]
                for qi, base in ((0, 0), (1, 4)):
                    x1, x2 = raw[k][2 * qi], raw[k][2 * qi + 1]
                    cosv, sinv = cs[k]
                    dec = qdec[hl] if qi == 0 else kdec[hl]
                    eng_a = "dve"
                    ta, tb_ = t1[2 * qi], t1[2 * qi + 1]
                    TA, TBb = T1[2 * qi], T1[2 * qi + 1]
                    p.op(eng_a, lambda e, ta=ta, x1=x1, cosv=cosv: e.tensor_tensor(out=ta, in0=x1, in1=cosv, op=ALU.mult), reads=[RAW[k]], writes=[TA])
                    p.op(eng_a, lambda e, tb_=tb_, x2=x2, sinv=sinv: e.tensor_tensor(out=tb_, in0=x2, in1=sinv, op=ALU.mult), reads=[RAW[k]], writes=[TBb])
                    p.op(eng_a, lambda e, ta=ta, tb_=tb_: e.tensor_tensor(out=ta, in0=ta, in1=tb_, op=ALU.subtract), reads=[TA, TBb], writes=[TA])
                    p.op("act", lambda e, ta=ta, o=R[base]: e.copy(out=o, in_=ta), reads=[TA], writes=[ROTQ[k] if qi == 0 else ROTK[k]])
                    p.op(eng_a, lambda e, ta=ta, o=R[base + 2], dec=dec: e.tensor_tensor(out=o, in0=ta, in1=dec, op=ALU.mult), reads=[TA, CB])
                    p.op(eng_a, lambda e, ta=ta, x1=x1, sinv=sinv: e.tensor_tensor(out=ta, in0=x1, in1=sinv, op=ALU.mult), reads=[RAW[k]], writes=[TA])
                    p.op(eng_a, lambda e, tb_=tb_, x2=x2, cosv=cosv: e.tensor_tensor(out=tb_, in0=x2, in1=cosv, op=ALU.mult), reads=[RAW[k]], writes=[TBb])
                    p.op(eng_a, lambda e, ta=ta, tb_=tb_: e.tensor_tensor(out=ta, in0=ta, in1=tb_, op=ALU.add), reads=[TA, TBb], writes=[TA])
                    p.op("act", lambda e, ta=ta, o=R[base + 1]: e.copy(out=o, in_=ta), reads=[TA])
                    p.op(eng_a, lambda e, ta=ta, o=R[base + 3], dec=dec: e.tensor_tensor(out=o, in0=ta, in1=dec, op=ALU.mult), reads=[TA, CB])
                    if qi == 1:
                        pass
                p.last(ROTQ[k], "dve", "act")
                p.last(ROTK[k], "dve", "act")
                qa, qb, qda, qdb, ka, kb_, kda, kdb = R
                def emit_tr(c, m):
                    cs_ = slice(c * 64, (c + 1) * 64)
                    tv = banks[m][0:64, 0:256].bitcast(BF16)
                    srcs = [kda[:, cs_], kdb[:, cs_], vT[k][0][:, cs_], vT[k][1][:, cs_]]
                    for si, src in enumerate(srcs):
                        p.op("pe", lambda e, tv=tv, si=si, src=src: e.transpose(out=tv[:, si * 128:(si + 1) * 128], in_=src, identity=ident_bf),
                             reads=[ROTQ[k], ROTK[k], RAW[k]], writes=[BT[m]] if si == 0 else [])
                    p.last(BT[m], "pe")
                    p.op("act", lambda e, m=m, tv=tv: e.copy(out=kv[m][0:64, :], in_=tv), reads=[BT[m]], writes=[KV[m]])

                emit_tr(0, n % 2)
                for c in range(8):
                    m = n % 2
                    cs_ = slice(c * 64, (c + 1) * 64)
                    if c + 1 < 8:
                        emit_tr(c + 1, 1 - m)
                    ub = banks[4 + m]
                    p.op("pe", lambda e, ub=ub, m=m: e.matmul(ub[:, 0:256], lhsT=kv[m][0:64, 0:128], rhs=kv[m][0:64, 256:512], start=True, stop=True),
                         reads=[KV[m]], writes=[BU[m]])
                    p.op("pe", lambda e, ub=ub, m=m: e.matmul(ub[:, 256:512], lhsT=kv[m][0:64, 128:256], rhs=kv[m][0:64, 256:512], start=True, stop=True),
                         reads=[KV[m]])
                    p.last(BU[m], "pe")
                    p.op("dve", lambda e, ub=ub, hl=hl: e.scalar_tensor_tensor(out=state, in0=state, scalar=cvec[:, CV_CD + hl:CV_CD + hl + 1],
                                                                              in1=ub[:, :], op0=ALU.mult, op1=ALU.add),
                         reads=[BU[m], STATE, cB], writes=[STATE])
                    p.op("act", lambda e, m=m: e.copy(out=stb[1 - m], in_=state), reads=[STATE], writes=[STB[1 - m]])
                    so = banks[2 + m]
                    p.op("pe", lambda e, so=so, cs_=cs_, ka=ka, qa=qa: e.matmul(so[0:64, 256:320], lhsT=ka[:, cs_], rhs=qa[:, cs_], start=True, stop=False),
                         reads=[ROTQ[k], ROTK[k]], writes=[BSO[m]])
                    p.op("pe", lambda e, so=so, cs_=cs_, kb_=kb_, qb=qb: e.matmul(so[0:64, 256:320], lhsT=kb_[:, cs_], rhs=qb[:, cs_], start=False, stop=True),
                         reads=[ROTQ[k], ROTK[k]])
                    p.last(BSO[m], "pe")
                    p.op("dve", lambda e, so=so, m=m, hl=hl: e.tensor_tensor(out=ssb[m][0:64, :], in0=so[0:64, 256:320],
                                                                             in1=intra[:, hl * 64:(hl + 1) * 64], op=ALU.mult),
                         reads=[BSO[m], CB], writes=[SSB[m]])
                    p.op("pe", lambda e, so=so, m=m: e.matmul(so[0:64, 0:256], lhsT=ssb[m][0:64, :], rhs=kv[m][0:64, 256:512], start=True, stop=False),
                         reads=[SSB[m], KV[m]], writes=[BSO[m]])
                    p.op("pe", lambda e, so=so, m=m, cs_=cs_, qda=qda: e.matmul(so[0:64, 0:256], lhsT=qda[:, cs_], rhs=stb[m][:, 0:256], start=False, stop=False),
                         reads=[ROTQ[k], STB[m]])
                    p.op("pe", lambda e, so=so, m=m, cs_=cs_, qdb=qdb: e.matmul(so[0:64, 0:256], lhsT=qdb[:, cs_], rhs=stb[m][:, 256:512], start=False, stop=True),
                         reads=[ROTQ[k], STB[m]])
                    p.last(BSO[m], "pe")
                    p.op("dve", lambda e, so=so, m=m: e.bn_stats(out=gst[m][0:64, :], in_=so[0:64, 0:256]), reads=[BSO[m]], writes=[GS[m]])
                    p.op("dve", lambda e, m=m: e.bn_aggr(out=gmv[m][0:64, :], in_=gst[m][0:64, :]), reads=[GS[m]], writes=[GS[m]])
                    p.op("act", lambda e, m=m: e.activation(out=gsd[m][0:64, :], in_=gmv[m][0:64, 1:2], func=AF.Sqrt, bias=eps_t[0:64, 0:1], scale=1.0),
                         reads=[GS[m]], writes=[GS[m]])
                    p.op("dve", lambda e, m=m: e.reciprocal(out=grs[m][0:64, :], in_=gsd[m][0:64, :]), reads=[GS[m]], writes=[GS[m]])
                    p.op("dve", lambda e, so=so, m=m, k=k, c=c: e.tensor_scalar(out=ynb[k][0:64, c * 256:(c + 1) * 256], in0=so[0:64, 0:256],
                                                                          scalar1=gmv[m][0:64, 0:1], scalar2=grs[m][0:64, 0:1],
                                                                          op0=ALU.subtract, op1=ALU.mult),
                         reads=[BSO[m], GS[m]], writes=[YNB[k]] if c == 0 else [])
                    n += 1
                p.last(YNB[k], "dve")
                yb = banks[6 + k][:, :].bitcast(BF16)
                for ec in range(2):
                    for c in range(8):
                        p.op("pe", lambda e, yb=yb, ec=ec, c=c, k=k: e.transpose(out=yb[:, ec * 512 + c * 64: ec * 512 + (c + 1) * 64],
                                                                             in_=ynb[k][0:64, c * 256 + ec * 128: c * 256 + (ec + 1) * 128],
                                                                             identity=ident_bf[0:64, 0:64]),
                             reads=[YNB[k]], writes=[BYT[k]] if (ec == 0 and c == 0) else [])
                p.last(BYT[k], "pe")
                for ec in range(2):
                    gi = CV_GN + hl * 2 + ec
                    p.op("dve", lambda e, yb=yb, ec=ec, gi=gi: e.tensor_scalar(out=yaf[ec], in0=yb[:, ec * 512:(ec + 1) * 512],
                                                                            scalar1=cvec[:, gi:gi + 1], scalar2=cvec[:, gi + 4:gi + 5],
                                                                            op0=ALU.mult, op1=ALU.add),
                         reads=[BYT[k], cB], writes=[YAF[ec]])
                    p.op("dve", lambda e, ec=ec, k=k: e.tensor_tensor(out=yts[k][:, ec * 512:(ec + 1) * 512], in0=yaf[ec],
                                                                      in1=sg[k][:, ec * 512:(ec + 1) * 512], op=ALU.mult),
                         reads=[YAF[ec], RAW[k]], writes=[YTS[k]] if ec == 0 else [])
                p.last(YTS[k], "dve")
                for ec in range(2):
                    p.dma("sp", cin_ap(hl * 2 + ec, tl * 512), yts[k][:, ec * 512:(ec + 1) * 512],
                          ysl[k], reads=[YTS[k]])
        barrier(p)

    if upto >= 2:
        stage_ret()

    def stage_fox():
        ar.reset(base_mark)
        balloc = BigAlloc()
        NQG = S // 512
        triu = ar.alloc(128, F32)
        stri = ar.alloc(NBK, F32, parts=NBK)
        maskT = ar.alloc(4 * 512, BF16)
        lfrow = ar.alloc(128, F32, parts=NBK)
        lfcol = ar.alloc(NBK, F32)
        rsum = ar.alloc(1, F32, parts=NBK)
        rsb = ar.alloc(128, F32, parts=NBK)
        ccol = ar.alloc(NBK, F32)
        crow_f = ar.alloc(128, F32, parts=NBK)
        cr_h = ar.alloc(128, BF16, parts=NBK)
        cr_r = ar.alloc(128, F32, parts=NBK)
        cr_m = ar.alloc(128, BF16, parts=NBK)
        cr_l = ar.alloc(128, BF16, parts=NBK)
        pT = [ar.alloc(512, BF16) for _ in range(3)]
        rl = ar.alloc(512, F32)
        of = ar.alloc(512, F32)
        acc = [ar.alloc(512, F32) for _ in range(2)]
        ACC = [Buf(), Buf()]
        yo = [ar.alloc(512, BF16) for _ in range(2)]
        qT = balloc(S, BF16)
        kT = balloc(S, BF16)
        vTt = balloc(S, BF16)
        sgT = balloc(S, F32)
        V = balloc(S, BF16)
        crow3 = balloc(S, BF16, parts=3)
        CB = Buf()
        s0 = p.slot()
        p.dma("sp", triu, triu_d, s0, writes=[CB])
        p.dma("sp", stri, stri_d, s0, writes=[CB])
        for r in range(4):
            p.dma("sp", maskT[:, r * 512:(r + 1) * 512], mask_d[r], s0, writes=[CB] if r == 0 else [])
        CB.w = {("d", s0.sid): s0.count}
        HB = Buf()
        LF = Buf()
        CC = Buf()
        CR = Buf()
        CR3 = Buf()
        CRD = Buf()
        VB = Buf()
        PT = [Buf() for _ in range(3)]
        BST = [Buf(), Buf()]
        BO = [Buf(), Buf()]
        BL = [Buf(), Buf()]
        BX = [Buf(), Buf()]
        RL = Buf()
        OFb = Buf()
        YO = [Buf(), Buf()]
        hsl = p.slot()
        lsl = p.slot()
        csl = p.slot()
        c3sl = p.slot()
        ysl = [p.slot() for _ in range(2)]

        for fl in range(4):
            p.dma("sp", qT, s_qf[fl], hsl, writes=[HB])
            p.dma("sp", kT, s_kf[fl], hsl)
            p.dma("sp", vTt, s_vf[fl], hsl)
            p.dma("sp", sgT, s_sgf[fl], hsl)
            HB.w = {("d", hsl.sid): hsl.count}
            p.dma("sp", lfrow, s_lf[fl].rearrange("(n q) -> n q", q=128), lsl, writes=[LF])
            bx = banks[6]
            p.op("pe", lambda e, bx=bx: e.transpose(out=bx[:, 0:NBK], in_=lfrow, identity=ident_f[0:NBK, 0:NBK]), reads=[LF, cB], writes=[BX[0]])
            p.op("dve", lambda e, bx=bx: e.tensor_copy(out=lfcol, in_=bx[:, 0:NBK]), reads=[BX[0]], writes=[CC])
            p.op("dve", lambda e: e.reduce_sum(out=rsum, in_=lfrow, axis=AX.X), reads=[LF], writes=[CC])
            p.op("dve", lambda e: e.tensor_scalar(out=rsb, in0=ones_f[0:NBK, :], scalar1=rsum[:, 0:1], scalar2=None, op0=ALU.mult),
                 reads=[CC], writes=[CC])
            bx1 = banks[7]
            p.op("pe", lambda e, bx1=bx1: e.matmul(bx1[:, 0:NBK], lhsT=triu, rhs=lfcol, start=True, stop=False), reads=[CC, CB], writes=[BX[1]])
            p.op("pe", lambda e, bx1=bx1: e.matmul(bx1[:, 0:NBK], lhsT=rsb, rhs=stri, start=False, stop=True), reads=[CC, CB])
            p.last(BX[1], "pe")
            p.op("dve", lambda e, bx1=bx1: e.tensor_copy(out=ccol, in_=bx1[:, 0:NBK]), reads=[BX[1]], writes=[CC])
            p.op("pe", lambda e, bx=bx: e.transpose(out=bx[0:NBK, 128:256], in_=ccol, identity=ident_f), reads=[CC], writes=[BX[0]])
            p.op("dve", lambda e, bx=bx: e.tensor_scalar(out=crow_f, in0=bx[0:NBK, 128:256], scalar1=-1.0, scalar2=None, op0=ALU.mult),
                 reads=[BX[0]], writes=[CR])
            p.op("dve", lambda e: e.tensor_copy(out=cr_h, in_=crow_f), reads=[CR], writes=[CR])
            p.op("dve", lambda e: e.tensor_tensor(out=cr_r, in0=crow_f, in1=cr_h, op=ALU.subtract), reads=[CR], writes=[CR])
            p.op("dve", lambda e: e.tensor_copy(out=cr_m, in_=cr_r), reads=[CR], writes=[CR])
            p.op("dve", lambda e: e.tensor_tensor(out=cr_r, in0=cr_r, in1=cr_m, op=ALU.subtract), reads=[CR], writes=[CR])
            p.op("dve", lambda e: e.tensor_copy(out=cr_l, in_=cr_r), reads=[CR], writes=[CR])
            for pi, src in enumerate((cr_h, cr_m, cr_l)):
                p.dma("sp", s_crow[fl, pi].rearrange("(n q) -> n q", q=128), src, csl, reads=[CR], writes=[CRD] if pi == 0 else [])
            CRD.w = {("d", csl.sid): csl.count}
            p.dma("sp", crow3, s_crow[fl], c3sl, reads=[CRD], writes=[CR3])
            ng8 = (NBK + 7) // 8
            for n8 in range(ng8):
                bb = banks[6 + (n8 % 2)][:, :].bitcast(BF16)
                nb8 = min(8, NBK - n8 * 8)
                for j in range(nb8):
                    blk = n8 * 8 + j
                    p.op("pe", lambda e, bb=bb, j=j, blk=blk: e.transpose(out=bb[:, j * 128:(j + 1) * 128],
                                                                      in_=vTt[:, blk * 128:(blk + 1) * 128], identity=ident_bf),
                         reads=[HB], writes=[BX[n8 % 2]] if j == 0 else [])
                p.last(BX[n8 % 2], "pe")
                if n8 % 2 == 0:
                    p.op("act", lambda e, bb=bb, n8=n8, nb8=nb8: e.copy(out=V[:, n8 * 1024:n8 * 1024 + nb8 * 128], in_=bb[:, 0:nb8 * 128]),
                         reads=[BX[n8 % 2]], writes=[VB] if n8 == 0 else [])
                else:
                    p.op("dve", lambda e, bb=bb, n8=n8, nb8=nb8: e.tensor_copy(out=V[:, n8 * 1024:n8 * 1024 + nb8 * 128], in_=bb[:, 0:nb8 * 128]),
                         reads=[BX[n8 % 2]])
            p.last(VB, "act", "dve")
            units = [(qg, kb) for qg in range(NQG) for kb in range(4 * qg + 4)]

            def qk(u):
                qg, kb = units[u]
                bs = banks[u % 2]
                diag = kb >= 4 * qg
                p.op("pe", lambda e, bs=bs, qg=qg, kb=kb: e.matmul(bs[:, :], lhsT=kT[:, kb * 128:(kb + 1) * 128],
                                                               rhs=qT[:, qg * 512:(qg + 1) * 512], start=True, stop=False),
                     reads=[HB], writes=[BST[u % 2]])
                p.op("pe", lambda e, bs=bs, qg=qg, diag=diag: e.matmul(bs[:, :], lhsT=ones_bf[0:3, :], rhs=crow3[:, qg * 512:(qg + 1) * 512],
                                                                      start=False, stop=not diag),
                     reads=[CR3])
                if diag:
                    r = kb - 4 * qg
                    p.op("pe", lambda e, bs=bs, r=r: e.matmul(bs[:, :], lhsT=ident_bf, rhs=maskT[:, r * 512:(r + 1) * 512], start=False, stop=True),
                         reads=[CB])
                p.last(BST[u % 2], "pe")

            def epilogue(qg):
                g2 = qg % 2
                p.op("pe", lambda e, g2=g2: e.matmul(banks[4 + g2][:, :], lhsT=ones_f, rhs=acc[g2], start=True, stop=True),
                     reads=[ACC[g2]], writes=[BL[g2]])
                p.op("dve", lambda e, g2=g2: e.reciprocal(out=rl, in_=banks[4 + g2][:, :]), reads=[BL[g2]], writes=[RL])
                p.op("dve", lambda e, g2=g2: e.tensor_tensor(out=of, in0=banks[2 + g2][:, :], in1=rl, op=ALU.mult), reads=[BO[g2], RL], writes=[OFb])
                yk = qg % 2
                p.op("dve", lambda e, yk=yk, qg=qg: e.tensor_tensor(out=yo[yk], in0=of, in1=sgT[:, qg * 512:(qg + 1) * 512], op=ALU.mult),
                     reads=[OFb, HB], writes=[YO[yk]])
                p.dma("sp", cin_ap(4 + fl, qg * 512), yo[yk], ysl[yk], reads=[YO[yk]])

            pending = []
            qk(0)
            for u, (qg, kb) in enumerate(units):
                if u + 1 < len(units):
                    qk(u + 1)
                bs = banks[u % 2]
                pk = u % 3
                p.op("act", lambda e, bs=bs, pk=pk, kb=kb: e.activation(out=pT[pk], in_=bs[:, :], func=AF.Exp, bias=ccol[:, kb:kb + 1], scale=1.0),
                     reads=[BST[u % 2], CC], writes=[PT[pk]])
                last = kb == 4 * qg + 3
                g2 = qg % 2
                p.op("pe", lambda e, g2=g2, kb=kb, pk=pk, last=last: e.matmul(banks[2 + g2][:, :], lhsT=V[:, kb * 128:(kb + 1) * 128], rhs=pT[pk],
                                                                         start=(kb == 0), stop=last),
                     reads=[PT[pk], VB], writes=[BO[g2]] if kb == 0 else [])
                if kb == 0:
                    p.op("dve", lambda e, g2=g2, pk=pk: e.tensor_copy(out=acc[g2], in_=pT[pk]), reads=[PT[pk]], writes=[ACC[g2]])
                else:
                    p.op("dve", lambda e, g2=g2, pk=pk: e.tensor_tensor(out=acc[g2], in0=acc[g2], in1=pT[pk], op=ALU.add),
                         reads=[PT[pk], ACC[g2]], writes=[ACC[g2]])
                if last:
                    p.last(BO[g2], "pe")
                    pending.append((u + 2, qg))
                while pending and pending[0][0] <= u:
                    epilogue(pending.pop(0)[1])
            while pending:
                epilogue(pending.pop(0)[1])
        barrier(p)

    if upto >= 3:
        stage_fox()

    XC = Buf()
    xsl = p.slot()
    p.pinned = {xsl.sid}
    if upto >= 3 and dbg:
        dsl = p.slot()
        p.dma("sp", cin_dbg, cin, dsl)
    if upto >= 4:
        for cq in range(4):
            for rb in range(8):
                p.custom("pool", lambda e, cq=cq, rb=rb: e.collective_compute("AllGather", ALU.bypass,
                                                                             replica_groups=[[0, 1, 2, 3], [4, 5, 6, 7]],
                                                                             ins=[cin[cq, rb * 128:(rb + 1) * 128, :]],
                                                                             outs=[cout[cq, rb * 512:(rb + 1) * 512, :]]), xsl, 1, writes=[XC])

    def sp_setup(e):
        reg = st.enter_context(e.register("tokoff"))
        e.reg_load(reg, tokoff_d[0:1, 0:1])
        p.env["tokoff"] = e.snap(reg, min_val=0, max_val=3)
    if upto >= 5:
        p.pre["sp"].append(sp_setup)

    def phaseB():
        stage_lnt(xoff, TB // 128, TB)
        ar.reset(base_mark)
        gof = [ar.alloc(512, F32) for _ in range(3)]
        GO = [Buf() for _ in range(3)]
        gsl = [p.slot() for _ in range(3)]
        wmB = ar.mark()
        gcnt = [0]

        def evacG(i, tg, bank, PSb):
            k = gcnt[0] % 3
            gcnt[0] += 1
            p.op("act", lambda e, k=k, bank=bank, i=i: e.activation(out=gof[k], in_=bank[:, :], func=AF.Sigmoid,
                                                                  bias=cvec[:, CV_BG + i:CV_BG + i + 1], scale=1.0),
                 reads=[PSb], writes=[GO[k]])
            p.dma("sp", s_gates[i, :, tg * 512:(tg + 1) * 512], gof[k], gsl[k], reads=[GO[k]])

        stage_proj([(wg[cb], 128, KC, list(range(KC))) for cb in range(NCG)], KC, TB, evacG, wmB)

        ar.reset(base_mark)
        aT = actT_view(32, TB)
        ysl2 = p.slot()
        p.wait_all("sp", [XC])
        p.pinned = set()
        tvv = actT_raw[:, 0:32 * TB].rearrange("p (r b t) -> p r b t", r=4, b=8)
        for rb in range(8):
            def fn(e, rb=rb):
                src = cout[bass.ds(p.env["tokoff"], 1), rb * 512:(rb + 1) * 512, :].rearrange("o (r p) t -> p (o r) t", p=128)
                return e.dma_start(out=tvv[:, :, rb, :], in_=src)
            ysl2.count += 16
            p.q["sp"].append([fn, {}, False, ysl2.sid])
        barrier(p)
        NG = TB // 512
        gbuf = [[ar.alloc(512, F32) for _ in range(NG)] for _ in range(2)]
        tmpb = [ar.alloc(512, F32) for _ in range(NG)]
        mf = [ar.alloc(512, F32) for _ in range(2)]
        mb = [ar.alloc(512, BF16) for _ in range(2)]
        GB = [[Buf() for _ in range(NG)] for _ in range(2)]
        TMP = [Buf() for _ in range(NG)]
        MF = [Buf() for _ in range(2)]
        MB = [Buf() for _ in range(2)]
        g2sl = [p.slot() for _ in range(2)]
        msl = [p.slot() for _ in range(2)]
        wm3 = ar.mark()
        blocks = []
        for cb in range(NCD):
            blocks.append((wbr[cb], 128, 16, [(i // 4) * 8 + (i % 4) for i in range(16)]))
            blocks.append((wbf[cb], 128, 16, [(i // 4) * 8 + 4 + (i % 4) for i in range(16)]))
        mcnt = [0]

        def preM(i):
            cb, br = divmod(i, 2)
            k = i % 2
            for tg in range(NG):
                p.dma("sp", gbuf[k][tg], s_gates[br * NCD + cb, :, tg * 512:(tg + 1) * 512], g2sl[k], writes=[GB[k][tg]])
            for tg in range(NG):
                GB[k][tg].w = {("d", g2sl[k].sid): g2sl[k].count}

        def evacM(i, tg, bank, PSb):
            cb, br = divmod(i, 2)
            k = i % 2
            if br == 0:
                p.op("dve", lambda e, tg=tg, k=k, bank=bank: e.tensor_tensor(out=tmpb[tg], in0=bank[:, :], in1=gbuf[k][tg], op=ALU.mult),
                     reads=[PSb, GB[k][tg]], writes=[TMP[tg]])
            else:
                j = mcnt[0] % 2
                mcnt[0] += 1
                p.op("dve", lambda e, tg=tg, k=k, bank=bank, j=j: e.tensor_tensor(out=mf[j], in0=bank[:, :], in1=gbuf[k][tg], op=ALU.mult),
                     reads=[PSb, GB[k][tg]], writes=[MF[j]])
                p.op("dve", lambda e, tg=tg, j=j: e.tensor_tensor(out=mb[j], in0=mf[j], in1=tmpb[tg], op=ALU.add),
                     reads=[MF[j], TMP[tg]], writes=[MB[j]])
                p.dma("sp", s_merged[cb, :, tg * 512:(tg + 1) * 512], mb[j], msl[j], reads=[MB[j]])

        stage_proj(blocks, 32, TB, evacM, wm3, pre=preM)

        ar.reset(base_mark)
        aT = actT_view(KC, TB)
        msl2 = p.slot()
        for c in range(KC):
            p.dma("sp", aT[:, c, :], s_merged[c], msl2)
        barrier(p)
        yoT = [ar.alloc(512, F32) for _ in range(2)]
        yst = [ar.alloc(512, F32) for _ in range(2)]
        YOT = [Buf() for _ in range(2)]
        YST = [Buf() for _ in range(2)]
        BTR = [Buf() for _ in range(2)]
        osl = [p.slot() for _ in range(2)]
        wm4 = ar.mark()
        ocnt = [0]

        def evacO(i, tg, bank, PSb):
            j = ocnt[0] % 2
            ocnt[0] += 1
            p.op("act", lambda e, j=j, bank=bank: e.copy(out=yoT[j], in_=bank[:, :]), reads=[PSb], writes=[YOT[j]])
            tb = banks[4 + j]
            for q in range(4):
                p.op("pe", lambda e, tb=tb, j=j, q=q: e.transpose(out=tb[:, q * 128:(q + 1) * 128], in_=yoT[j][:, q * 128:(q + 1) * 128], identity=ident_f),
                     reads=[YOT[j]], writes=[BTR[j]] if q == 0 else [])
            p.last(BTR[j], "pe")
            p.op("dve", lambda e, tb=tb, j=j: e.tensor_copy(out=yst[j], in_=tb[:, :]), reads=[BTR[j]], writes=[YST[j]])
            p.dma("sp", s_yo[tg * 512:(tg + 1) * 512, i * 128:(i + 1) * 128].rearrange("(q p) f -> p q f", p=128),
                  yst[j].rearrange("p (q f) -> p q f", q=4), osl[j], reads=[YST[j]])

        stage_proj([(wo[cb], 128, KC, list(range(KC))) for cb in range(NCD)], KC, TB, evacO, wm4)

        ar.reset(base_mark)
        balloc = BigAlloc()
        gin = balloc(D, F32)
        bin_ = balloc(D, F32)
        gpo = balloc(D, F32)
        bpo = balloc(D, F32)
        RB = Buf()
        rsl = p.slot()
        for i, t in enumerate((gin, bin_, gpo, bpo)):
            p.dma("sp", t, rows_d[i * D:(i + 1) * D].partition_broadcast(128), rsl, writes=[RB] if i == 0 else [])
        RB.w = {("d", rsl.sid): rsl.count}
        xb = [ar.alloc(D, F32) for _ in range(2)]
        yb = [ar.alloc(D, F32) for _ in range(2)]
        nch = max(D // 512, 1)
        cw = D // nch
        stt = [ar.alloc(nch * 6, F32) for _ in range(2)]
        mv = [ar.alloc(2, F32) for _ in range(2)]
        sd = [ar.alloc(1, F32) for _ in range(2)]
        rs = [ar.alloc(1, F32) for _ in range(2)]
        XB = [Buf() for _ in range(2)]
        YB = [Buf() for _ in range(2)]
        SM = [Buf() for _ in range(2)]
        xsl_ = [p.slot() for _ in range(2)]
        ysl_ = [p.slot() for _ in range(2)]
        osl2 = [p.slot() for _ in range(2)]
        ntt = TB // 128

        def loadB4(tt, k):
            p.dma("sp", xb[k], xoff[tt * 128:(tt + 1) * 128, :], xsl_[k], writes=[XB[k]])
            p.dma("sp", yb[k], s_yo[tt * 128:(tt + 1) * 128, :], ysl_[k], writes=[YB[k]])

        def lnstats(k, src, SRCB):
            for i in range(nch):
                p.op("dve", lambda e, k=k, i=i, src=src: e.bn_stats(out=stt[k][:, i * 6:(i + 1) * 6], in_=src[:, i * cw:(i + 1) * cw]),
                     reads=[SRCB], writes=[SM[k]] if i == 0 else [])
            p.op("dve", lambda e, k=k: e.bn_aggr(out=mv[k], in_=stt[k]), reads=[SM[k]], writes=[SM[k]])
            p.op("act", lambda e, k=k: e.activation(out=sd[k], in_=mv[k][:, 1:2], func=AF.Sqrt, bias=eps_t[:, 0:1], scale=1.0),
                 reads=[SM[k]], writes=[SM[k]])
            p.op("dve", lambda e, k=k: e.reciprocal(out=rs[k], in_=sd[k]), reads=[SM[k]], writes=[SM[k]])

        nm = [ar.alloc(1, F32) for _ in range(2)]

        def norm_act(k, buf, BUFB):
            p.op("dve", lambda e, k=k: e.tensor_scalar(out=nm[k], in0=mv[k][:, 0:1], scalar1=rs[k][:, 0:1], scalar2=-1.0,
                                                       op0=ALU.mult, op1=ALU.mult), reads=[SM[k]], writes=[SM[k]])
            p.op("act", lambda e, k=k, buf=buf: e.activation(out=buf, in_=buf, func=AF.Identity, scale=rs[k][:, 0:1], bias=nm[k][:, 0:1]),
                 reads=[BUFB, SM[k]], writes=[BUFB])

        loadB4(0, 0)
        for tt in range(ntt):
            k = tt % 2
            if tt + 1 < ntt:
                loadB4(tt + 1, 1 - k)
            lnstats(k, xb[k], XB[k])
            norm_act(k, xb[k], XB[k])
            p.op("dve", lambda e, k=k: e.tensor_tensor(out=xb[k], in0=xb[k], in1=gin, op=ALU.mult), reads=[XB[k], RB], writes=[XB[k]])
            p.op("dve", lambda e, k=k: e.tensor_tensor(out=xb[k], in0=xb[k], in1=bin_, op=ALU.add), reads=[XB[k], RB], writes=[XB[k]])
            p.op("dve", lambda e, k=k: e.scalar_tensor_tensor(out=yb[k], in0=xb[k], scalar=float(cfg.alpha), in1=yb[k],
                                                              op0=ALU.mult, op1=ALU.add), reads=[XB[k], YB[k]], writes=[YB[k]])
            lnstats(k, yb[k], YB[k])
            norm_act(k, yb[k], YB[k])
            p.op("dve", lambda e, k=k: e.tensor_tensor(out=yb[k], in0=yb[k], in1=gpo, op=ALU.mult), reads=[YB[k], RB], writes=[YB[k]])
            p.op("dve", lambda e, k=k: e.tensor_tensor(out=yb[k], in0=yb[k], in1=bpo, op=ALU.add), reads=[YB[k], RB], writes=[YB[k]])
            p.dma("sp", out[tt * 128:(tt + 1) * 128, :], yb[k], osl2[k], reads=[YB[k]])
        barrier(p)

    if upto >= 5:
        phaseB()
    barrier(p)
    p.replay(st)
    return nc, st, p

import ml_dtypes
from concourse.bass_utils import run_bass_kernel_spmd

_PROG_CACHE = {}


def _tile_w(wblk, M):
    K = wblk.shape[0]
    return np.ascontiguousarray(wblk.reshape(K // 128, 128, M).transpose(1, 0, 2).reshape(128, (K // 128) * M))


def _consts(cfg):
    S, NBK = cfg.S, cfg.NBK
    half = RET_HD // 2
    inv_freq = (10000.0 ** (-np.arange(half, dtype=np.float32) / np.float32(half))).astype(np.float32)
    pos = np.arange(S, dtype=np.float32)
    ang = (pos[:, None] * inv_freq[None, :]).astype(np.float32)
    c = {}
    c["cos"] = np.ascontiguousarray(np.cos(ang.astype(np.float64)).astype(np.float32).T)
    c["sin"] = np.ascontiguousarray(np.sin(ang.astype(np.float64)).astype(np.float32).T)
    c["ident_bf"] = np.eye(128, dtype=np.float32).astype(ml_dtypes.bfloat16)
    c["ident_f"] = np.eye(128, dtype=np.float32)
    c["triu"] = np.triu(np.ones((128, 128), np.float32))
    c["stri"] = np.triu(np.ones((NBK, NBK), np.float32), k=1)
    kk = np.arange(128)[None, :, None] + 128 * np.arange(4)[:, None, None]
    qq = np.arange(512)[None, None, :]
    c["maskT"] = np.where(kk > qq, -30000.0, 0.0).astype(np.float32).astype(ml_dtypes.bfloat16)
    return c


def _ret_consts(heads):
    out = {}
    idx = np.arange(64, dtype=np.float64)
    intra, qd, kd, cd = [], [], [], []
    for h in heads:
        lg = np.log1p(-(2.0 ** (-5.0 - h)))
        sc = RET_HD ** -0.5
        intra.append(np.exp(lg * np.abs(idx[:, None] - idx[None, :])) * sc)
        qrow = np.exp(lg * idx)
        krow = np.exp(lg * (64 - idx)) * sc
        qd.append(np.broadcast_to(np.tile(qrow, 8)[None, :], (128, 512)))
        kd.append(np.broadcast_to(np.tile(krow, 8)[None, :], (128, 512)))
        cd.append(np.exp(lg * 64))
    out["intra"] = np.ascontiguousarray(np.stack(intra).astype(np.float32))
    out["qdec"] = np.ascontiguousarray(np.stack(qd).astype(np.float32))
    out["kdec"] = np.ascontiguousarray(np.stack(kd).astype(np.float32))
    return out, cd


def make_in_maps(cfg, x, ln_in_g, ln_in_b, w_in, b_gate, b_forget, ret_norm_g, ret_norm_b,
                 w_branch_ret, w_branch_fox, w_out, ln_post_g, ln_post_b):
    D, S, KC, TB = cfg.D, cfg.S, cfg.KC, cfg.TB
    NCG, NCD = 2 * D // 128, D // 128
    f32 = np.float32
    x = np.asarray(x, f32)
    w_in = np.asarray(w_in, f32)[0]
    RW, FW = 2048, 2048
    o_qr, o_kr, o_vr, o_gr = 0, RW, 2 * RW, 3 * RW
    o_qf, o_kf, o_vf, o_gf = 4 * RW, 4 * RW + FW, 4 * RW + 2 * FW, 4 * RW + 3 * FW
    o_fl = 4 * RW + 4 * FW
    o_gate = o_fl + 16
    consts = _consts(cfg)
    wg = np.stack([_tile_w(w_in[:, o_gate + cb * 128:o_gate + (cb + 1) * 128], 128) for cb in range(NCG)])
    wbr_full = np.asarray(w_branch_ret, f32)[0]
    wbf_full = np.asarray(w_branch_fox, f32)[0]
    wo_full = np.asarray(w_out, f32)[0]
    wbr = np.stack([_tile_w(wbr_full[:, cb * 128:(cb + 1) * 128], 128) for cb in range(NCD)])
    wbf = np.stack([_tile_w(wbf_full[:, cb * 128:(cb + 1) * 128], 128) for cb in range(NCD)])
    wo = np.stack([_tile_w(wo_full[:, cb * 128:(cb + 1) * 128], 128) for cb in range(NCD)])
    g_in, b_in = np.asarray(ln_in_g, f32), np.asarray(ln_in_b, f32)
    g_po, b_po = np.asarray(ln_post_g, f32)[0], np.asarray(ln_post_b, f32)[0]
    bg = np.asarray(b_gate, f32)[0]
    bfg = np.asarray(b_forget, f32)[0]
    rg, rb = np.asarray(ret_norm_g, f32)[0], np.asarray(ret_norm_b, f32)[0]
    in_maps = []
    for core in range(8):
        b, g = divmod(core, 4)
        m = dict(consts)
        m["x"] = np.ascontiguousarray(x[b])
        m["xoff_rows"] = np.ascontiguousarray(x[b, g * TB:(g + 1) * TB])
        m["tokoff"] = np.array([[g]], np.int32)
        blocks = []
        for hl in range(2):
            hh = 2 * g + hl
            for off in (o_qr, o_kr, o_vr, o_gr):
                for ec in range(2):
                    c0 = off + hh * 256 + ec * 128
                    blocks.append(_tile_w(w_in[:, c0:c0 + 128], 128))
        for fl in range(4):
            fh = 4 * g + fl
            for off in (o_qf, o_kf, o_vf, o_gf):
                c0 = off + fh * 128
                blocks.append(_tile_w(w_in[:, c0:c0 + 128], 128))
        m["wa"] = np.stack(blocks)
        m["wf"] = _tile_w(w_in[:, o_fl + 4 * g:o_fl + 4 * g + 4], 4)
        m["wg"], m["wbr"], m["wbf"], m["wo"] = wg, wbr, wbf, wo
        rc, cd = _ret_consts([2 * g, 2 * g + 1])
        m.update(rc)
        NV = 2 * KC + NCG + 2 + 1 + 8
        cv = np.zeros((128, NV), f32)
        cv[:, 0:KC] = g_in.reshape(KC, 128).T
        cv[:, KC:2 * KC] = b_in.reshape(KC, 128).T
        cv[:, 2 * KC:2 * KC + NCG] = bg.reshape(NCG, 128).T
        cv[:, 2 * KC + NCG] = cd[0]
        cv[:, 2 * KC + NCG + 1] = cd[1]
        cv[0:4, 2 * KC + NCG + 2] = bfg[4 * g:4 * g + 4]
        for hl in range(2):
            for ec in range(2):
                f0 = (2 * g + hl) * 256 + ec * 128
                cv[:, 2 * KC + NCG + 3 + hl * 2 + ec] = rg[f0:f0 + 128]
                cv[:, 2 * KC + NCG + 3 + 4 + hl * 2 + ec] = rb[f0:f0 + 128]
        m["cvec"] = cv
        m["rows"] = np.concatenate([g_in, b_in, g_po, b_po, rg[2 * g * 256:(2 * g + 2) * 256], rb[2 * g * 256:(2 * g + 2) * 256]]).astype(f32)
        in_maps.append(m)
    return in_maps


def run(cfg, inputs, dbg=False, upto=99, trace=False):
    key = (cfg.D, cfg.S, cfg.TS, dbg, upto)
    if key not in _PROG_CACHE:
        nc, st, p = build_program(cfg, dbg=dbg, upto=upto)
        _PROG_CACHE[key] = (nc, st, p)
    nc, st, p = _PROG_CACHE[key]
    in_maps = make_in_maps(cfg, **inputs)
    res = run_bass_kernel_spmd(nc, in_maps, core_ids=list(range(8)), trace=trace)
    out = np.zeros((2, cfg.S, cfg.D), np.float32)
    for core in range(8):
        b, g = divmod(core, 4)
        out[b, g * cfg.TB:(g + 1) * cfg.TB] = res.results[core]["out"]
    return out, res


def kernel(**inputs):
    cfg = Cfg(D=4096, S=8192, TS=2048)
    out, _ = run(cfg, inputs)
    return out
```

```python
import numpy as np
import concourse.bass as bass
import concourse.mybir as mybir

F32 = mybir.dt.float32
BF16 = mybir.dt.bfloat16
U8 = mybir.dt.uint8
AF = mybir.ActivationFunctionType
ALU = mybir.AluOpType
AX = mybir.AxisListType

ENG_NAMES = ("pe", "act", "dve", "pool", "sp")


class Buf:
    __slots__ = ("name", "w", "r")

    def __init__(self, name=""):
        self.name = name
        self.w = {}
        self.r = {}


def _merge(dst, src):
    for k, v in src.items():
        if dst.get(k, -1) < v:
            dst[k] = v


class Slot:
    __slots__ = ("sid", "count")

    def __init__(self, sid):
        self.sid = sid
        self.count = 0


class Prog:
    def __init__(self, nc):
        self.nc = nc
        self.q = {e: [] for e in ENG_NAMES}
        self.nslots = 0
        self._slots = []
        self.free = []
        self.pre = {e: [] for e in ENG_NAMES}
        self.env = {}

    def slot(self):
        if self.free:
            return self.free.pop()
        s = Slot(self.nslots)
        self.nslots += 1
        self._slots.append(s)
        return s

    def _deps(self, eng, reads, writes):
        deps = {}
        for b in reads:
            _merge(deps, b.w)
        for b in writes:
            _merge(deps, b.w)
            _merge(deps, b.r)
        if eng == "pe":
            deps.pop(("c", "pe"), None)
        return deps

    def op(self, eng, fn, reads=(), writes=()):
        idx = len(self.q[eng])
        deps = self._deps(eng, reads, writes)
        self.q[eng].append([fn, deps, False, None])
        key = ("c", eng)
        for b in reads:
            if b.r.get(key, -1) < idx:
                b.r[key] = idx
        for b in writes:
            b.w = {key: idx}
            b.r = {}
        return (eng, idx)

    def dma(self, eng, out, in_, slot, reads=(), writes=(), **kw):
        deps = self._deps(eng, reads, writes)
        slot.count += 16
        val = slot.count

        def fn(e, out=out, in_=in_, kw=kw):
            return e.dma_start(out=out, in_=in_, **kw)

        self.q[eng].append([fn, deps, False, slot.sid])
        key = ("d", slot.sid)
        for b in reads:
            if b.r.get(key, -1) < val:
                b.r[key] = val
        for b in writes:
            b.w = {key: val}
            b.r = {}

    def custom(self, eng, fn, slot, inc, reads=(), writes=()):
        deps = self._deps(eng, reads, writes)
        slot.count += inc
        val = slot.count
        self.q[eng].append([fn, deps, False, (slot.sid, inc)])
        key = ("d", slot.sid)
        for b in reads:
            if b.r.get(key, -1) < val:
                b.r[key] = val
        for b in writes:
            b.w = {key: val}
            b.r = {}

    def wait_all(self, eng, bufs):
        deps = {}
        for b in bufs:
            _merge(deps, b.w)
            _merge(deps, b.r)
        self.q[eng].append([None, deps, False, None])

    def last(self, buf, *engs):
        buf.w = {("c", e): len(self.q[e]) - 1 for e in engs}
        buf.r = {}

    def n_instr(self):
        return {e: len(v) for e, v in self.q.items()}

    def replay(self, stack):
        nc = self.nc
        for e in ENG_NAMES:
            for rec in self.q[e]:
                for k, v in rec[1].items():
                    if k[0] == "c":
                        self.q[k[1]][v][2] = True
        prefix = {}
        for e in ENG_NAMES:
            c = 0
            pl = []
            for rec in self.q[e]:
                if rec[2]:
                    c += 1
                pl.append(c)
            prefix[e] = pl
        csem = {e: stack.enter_context(nc.semaphore("c_" + e)) for e in ENG_NAMES if e != "sp"}
        dsem = [stack.enter_context(nc.semaphore("d_%d" % i)) for i in range(self.nslots)]
        block = stack.enter_context(nc.Block())
        q = self.q

        def run(ename, eng):
            waited = {}
            for f in self.pre[ename]:
                f(eng)
            for rec in q[ename]:
                fn, deps, sig, dm = rec
                for k, v in deps.items():
                    if k[0] == "c":
                        sem = csem[k[1]]
                        val = prefix[k[1]][v]
                        wk = k[1]
                    else:
                        sem = dsem[k[1]]
                        val = v
                        wk = k
                    if waited.get(wk, -1) >= val:
                        continue
                    waited[wk] = val
                    eng.wait_ge(sem, val)
                if fn is None:
                    continue
                ins = fn(eng)
                if dm is not None:
                    if isinstance(dm, tuple):
                        ins.then_inc(dsem[dm[0]], dm[1])
                    else:
                        ins.then_inc(dsem[dm], 16)
                elif sig:
                    ins.then_inc(csem[ename], 1)

        @block.tensor
        def _(e):
            run("pe", e)

        @block.scalar
        def _(e):
            run("act", e)

        @block.vector
        def _(e):
            run("dve", e)

        @block.gpsimd
        def _(e):
            run("pool", e)

        @block.sync
        def _(e):
            run("sp", e)


class Arena:
    def __init__(self, nc, stack, nbytes, name="arena"):
        self.t = stack.enter_context(nc.sbuf_tensor(name, [128, nbytes], U8))
        self.nbytes = nbytes
        self.off = 0

    def alloc(self, cols, dtype, parts=128):
        esz = 4 if dtype == F32 else 2
        nb = cols * esz
        a = (self.off + 63) // 64 * 64
        assert a + nb <= self.nbytes, ("arena overflow", a + nb, self.nbytes)
        self.off = a + nb
        v = self.t[0:parts, a:a + nb].bitcast(dtype)
        return v

    def mark(self):
        return self.off

    def reset(self, m):
        self.off = m

from contextlib import ExitStack

RET_HD = 256
FOX_HD = 128
LN_EPS = 1e-5
I32 = mybir.dt.int32


class Cfg:
    def __init__(self, D=4096, S=8192, TS=2048, depth=1):
        self.D, self.S, self.TS = D, S, TS
        self.KC = D // 128
        self.TB = S // 4
        self.NBK = S // 128
        self.alpha = (2.0 * depth) ** 0.25


def barrier(p):
    deps = {}
    for e in ENG_NAMES:
        if e == "sp":
            continue
        for i in range(len(p.q[e]) - 1, -1, -1):
            if p.q[e][i][0] is not None and p.q[e][i][3] is None:
                deps[("c", e)] = i
                break
    pinned = getattr(p, "pinned", set())
    for s in p._slots:
        if s.count and s.sid not in pinned:
            deps[("d", s.sid)] = s.count
    for e in ENG_NAMES:
        d = dict(deps)
        if e == "pe":
            d.pop(("c", "pe"), None)
        p.q[e].append([None, d, False, None])
    p.free = [s for s in p._slots if s.sid not in pinned]


def build_program(cfg, dbg=False, upto=99):
    D, S, TS, KC, TB, NBK = cfg.D, cfg.S, cfg.TS, cfg.KC, cfg.TB, cfg.NBK
    NCG = 2 * D // 128
    NCD = D // 128
    NGP = 2
    nc = bass.Bass("TRN2", target_bir_lowering=False)

    def din(name, shape, dt=F32):
        return nc.dram_tensor(name, list(shape), dt, kind="ExternalInput").ap()

    def dscr(name, shape, dt=F32, force_internal=False):
        kind = "ExternalOutput" if (dbg and not force_internal) else "Internal"
        return nc.dram_tensor(name, list(shape), dt, kind=kind).ap()

    x = din("x", [S, D])
    xoff = din("xoff_rows", [TB, D])
    tokoff_d = din("tokoff", [1, 1], I32)
    wa = din("wa", [32, 128, KC * 128])
    wf = din("wf", [128, KC * 4])
    wg = din("wg", [NCG, 128, KC * 128])
    wbr = din("wbr", [NCD, 128, 16 * 128])
    wbf = din("wbf", [NCD, 128, 16 * 128])
    wo = din("wo", [NCD, 128, KC * 128])
    NV = 2 * KC + NCG + 2 + 1 + 8
    cvec_d = din("cvec", [128, NV])
    rows_d = din("rows", [4 * D + 1024])
    cos_d = din("cos", [128, S])
    sin_d = din("sin", [128, S])
    identb_d = din("ident_bf", [128, 128], BF16)
    identf_d = din("ident_f", [128, 128])
    triu_d = din("triu", [128, 128])
    stri_d = din("stri", [NBK, NBK])
    mask_d = din("maskT", [4, 128, 512], BF16)
    intra_d = din("intra", [2, 64, 64])
    qdec_d = din("qdec", [2, 128, 512])
    kdec_d = din("kdec", [2, 128, 512])

    s_qk = dscr("s_qk", [2, 4, 128, S])
    s_vr = dscr("s_vr", [2, 2, 128, S], BF16)
    s_sgr = dscr("s_sgr", [2, 2, 128, S])
    s_qf = dscr("s_qf", [4, 128, S], BF16)
    s_kf = dscr("s_kf", [4, 128, S], BF16)
    s_vf = dscr("s_vf", [4, 128, S], BF16)
    s_sgf = dscr("s_sgf", [4, 128, S])
    s_lf = dscr("s_lf", [4, S])
    s_crow = dscr("s_crow", [4, 3, S], BF16)
    cin = dscr("cin", [4, 1024, TB], BF16, force_internal=True)
    cout = dscr("cout", [4, 4096, TB], BF16, force_internal=True)
    cin_dbg = dscr("cin_dbg", [4, 1024, TB], BF16) if dbg else None

    def cin_ap(rb, t0):
        cq, off = divmod(t0, TB)
        return cin[cq, rb * 128:(rb + 1) * 128, off:off + 512]
    s_gates = dscr("s_gates", [NCG, 128, TB])
    s_merged = dscr("s_merged", [NCD, 128, TB], BF16)
    s_yo = dscr("s_yo", [TB, D])
    out = nc.dram_tensor("out", [TB, D], F32, kind="ExternalOutput").ap()
    OUTB = Buf("out")

    st = ExitStack()
    p = Prog(nc)
    ar = Arena(nc, st, 206 * 1024)
    banks = [st.enter_context(nc.psum_tensor("bank%d" % i, [128, 512], F32)) for i in range(8)]

    ident_bf = ar.alloc(128, BF16)
    ident_f = ar.alloc(128, F32)
    ones_bf = ar.alloc(128, BF16)
    ones_f = ar.alloc(128, F32)
    cvec = ar.alloc(NV, F32)
    negbf = ar.alloc(1, F32)
    eps_t = ar.alloc(1, F32)
    one_t = ar.alloc(1, F32)
    cB = Buf("consts")
    s_c = p.slot()
    p.dma("sp", ident_bf, identb_d, s_c, writes=[cB])
    p.dma("sp", ident_f, identf_d, s_c, writes=[cB])
    p.dma("sp", cvec, cvec_d, s_c, writes=[cB])
    p.op("dve", lambda e: e.memset(ones_bf, 1.0))
    p.op("dve", lambda e: e.memset(ones_f, 1.0))
    p.op("dve", lambda e: e.memset(eps_t, LN_EPS))
    p.op("dve", lambda e: e.memset(one_t, 1.0))
    CV_G, CV_B, CV_BG, CV_CD, CV_BF = 0, KC, 2 * KC, 2 * KC + NCG, 2 * KC + NCG + 2
    CV_GN = CV_BF + 1
    p.op("dve", lambda e: e.tensor_scalar(out=negbf[0:4, :], in0=cvec[0:4, CV_BF:CV_BF + 1], scalar1=-1.0,
                                          scalar2=None, op0=ALU.mult), reads=[cB])
    ACT_BYTES = max(32 * max(TS, TB) * 2, 112 * 1024)
    actT_raw = ar.alloc(ACT_BYTES // 2, BF16)
    base_mark = ar.mark()
    barrier(p)

    def actT_view(nchunk, T):
        return actT_raw[:, 0:nchunk * T].rearrange("p (k t) -> p k t", k=nchunk)

    class BigAlloc:
        def __init__(self):
            self.off = 0

        def __call__(self, cols, dt, parts=128):
            esz = 2 if dt == BF16 else 4
            n16 = cols * esz // 2
            a = self.off
            self.off += (n16 + 31) // 32 * 32
            assert self.off * 2 <= ACT_BYTES, ("actT region overflow", self.off * 2, ACT_BYTES)
            v = actT_raw[0:parts, a:a + n16]
            return v if dt == BF16 else v.bitcast(F32)

    def stage_lnt(xsrc, n_tt, T):
        ar.reset(base_mark)
        aT = actT_view(KC, T)
        xb = [ar.alloc(D, F32) for _ in range(4)]
        nch = max(D // 512, 1)
        cw = D // nch
        stt = [ar.alloc(nch * 6, F32) for _ in range(4)]
        mv = [ar.alloc(2, F32) for _ in range(4)]
        sd = [ar.alloc(1, F32) for _ in range(4)]
        rs = [ar.alloc(1, F32) for _ in range(4)]
        XB = [Buf() for _ in range(4)]
        SM = [Buf() for _ in range(4)]
        PSB = [Buf() for _ in range(4)]
        sl = [p.slot() for _ in range(4)]

        def load(tt):
            k = tt % 4
            p.dma("sp", xb[k], xsrc[tt * 128:(tt + 1) * 128, :], sl[k], writes=[XB[k]])

        load(0)
        if n_tt > 1:
            load(1)
        for tt in range(n_tt):
            k = tt % 4
            if tt + 2 < n_tt:
                load(tt + 2)
            for i in range(nch):
                p.op("dve", lambda e, k=k, i=i: e.bn_stats(out=stt[k][:, i * 6:(i + 1) * 6], in_=xb[k][:, i * cw:(i + 1) * cw]),
                     reads=[XB[k]], writes=[SM[k]] if i == 0 else [])
            p.op("dve", lambda e, k=k: e.bn_aggr(out=mv[k], in_=stt[k]), reads=[SM[k]], writes=[SM[k]])
            p.op("act", lambda e, k=k: e.activation(out=sd[k], in_=mv[k][:, 1:2], func=AF.Sqrt, bias=eps_t[:, 0:1], scale=1.0),
                 reads=[SM[k]], writes=[SM[k]])
            p.op("dve", lambda e, k=k: e.reciprocal(out=rs[k], in_=sd[k]), reads=[SM[k]], writes=[SM[k]])
            p.op("dve", lambda e, k=k: e.tensor_scalar(out=sd[k], in0=mv[k][:, 0:1], scalar1=rs[k][:, 0:1], scalar2=-1.0,
                                                       op0=ALU.mult, op1=ALU.mult), reads=[SM[k]], writes=[SM[k]])
            p.op("act", lambda e, k=k: e.activation(out=xb[k], in_=xb[k], func=AF.Identity, scale=rs[k][:, 0:1], bias=sd[k][:, 0:1]),
                 reads=[XB[k], SM[k]], writes=[XB[k]])
            pair, j = divmod(tt, 2)
            if j == 1 or tt == n_tt - 1:
                nj = j + 1
                ks = [(tt - j + jj) % 4 for jj in range(nj)]
                for kc in range(KC):
                    bk = kc % 4
                    pv = banks[bk][:, 0:128 * nj]
                    for jj in range(nj):
                        p.op("pe", lambda e, kk=ks[jj], jj=jj, kc=kc, pv=pv: e.transpose(out=pv[:, jj * 128:(jj + 1) * 128],
                                                                                      in_=xb[kk][:, kc * 128:(kc + 1) * 128],
                                                                                      identity=ident_f),
                             reads=[XB[ks[jj]]], writes=[PSB[bk]] if jj == 0 else [])
                    p.last(PSB[bk], "pe")
                    dst = aT[:, kc, pair * 256: pair * 256 + 128 * nj]
                    if kc % 2 == 0:
                        p.op("act", lambda e, pv=pv, dst=dst, kc=kc: e.activation(out=dst, in_=pv, func=AF.Identity,
                                                                                 scale=cvec[:, CV_G + kc:CV_G + kc + 1],
                                                                                 bias=cvec[:, CV_B + kc:CV_B + kc + 1]),
                             reads=[PSB[bk]])
                    else:
                        p.op("dve", lambda e, pv=pv, dst=dst, kc=kc: e.tensor_scalar(out=dst, in0=pv, scalar1=cvec[:, CV_G + kc:CV_G + kc + 1],
                                                                                    scalar2=cvec[:, CV_B + kc:CV_B + kc + 1],
                                                                                    op0=ALU.mult, op1=ALU.add),
                             reads=[PSB[bk]])
        barrier(p)

    def stage_proj(blocks, nchunk, T, evac, work_mark, pre=None):
        aT = actT_view(nchunk, T)
        NG = T // 512
        ngp = min(NGP, NG)
        npass = NG // ngp
        ar.reset(work_mark)
        wsz = max(nk * M for (_, M, nk, _) in blocks)
        wf32 = [ar.alloc(wsz, F32) for _ in range(2)]
        wb16 = [ar.alloc(wsz, BF16) for _ in range(2)]
        WF = [Buf() for _ in range(2)]
        WB = [Buf() for _ in range(2)]
        WB2 = [Buf() for _ in range(2)]
        PS = [[Buf() for _ in range(ngp)] for _ in range(2)]
        sl = [p.slot() for _ in range(2)]

        def load(i):
            w_ap, M, nk, _ = blocks[i]
            k = i % 2
            p.dma("sp", wf32[k][:, 0:nk * M], w_ap, sl[k], writes=[WF[k]])
            half = (nk * M) // 2
            p.op("dve", lambda e, k=k, half=half: e.tensor_copy(out=wb16[k][:, 0:half], in_=wf32[k][:, 0:half]),
                 reads=[WF[k]], writes=[WB[k]])
            p.op("act", lambda e, k=k, half=half, n=nk * M: e.copy(out=wb16[k][:, half:n], in_=wf32[k][:, half:n]),
                 reads=[WF[k]], writes=[WB2[k]])

        load(0)
        u = 0
        for i, (w_ap, M, nk, kmap) in enumerate(blocks):
            k = i % 2
            if i + 1 < len(blocks):
                load(i + 1)
            if pre is not None:
                pre(i)
            for ps in range(npass):
                sset = u % 2
                u += 1
                for kc in range(nk):
                    for t in range(ngp):
                        tg = ps * ngp + t
                        bank = banks[sset * ngp + t]
                        p.op("pe", lambda e, bank=bank, k=k, kc=kc, tg=tg, M=M, nk=nk, kmap=kmap: e.matmul(
                            bank[0:M, :], lhsT=wb16[k][:, kc * M:(kc + 1) * M], rhs=aT[:, kmap[kc], tg * 512:(tg + 1) * 512],
                            start=(kc == 0), stop=(kc == nk - 1)),
                            reads=[WB[k], WB2[k]], writes=[PS[sset][t]] if kc == 0 else [])
                for t in range(ngp):
                    p.last(PS[sset][t], "pe")
                    evac(i, ps * ngp + t, banks[sset * ngp + t], PS[sset][t])
        barrier(p)

    A_blocks = [(wa[cb], 128, KC, list(range(KC))) for cb in range(32)]
    A_blocks.append((wf, 4, KC, list(range(KC))))

    def phaseA_proj(t0):
        ar.reset(base_mark)
        of32 = [ar.alloc(512, F32) for _ in range(3)]
        ob16 = [ar.alloc(512, BF16) for _ in range(3)]
        tmpf = [ar.alloc(512, F32) for _ in range(2)]
        OF = [Buf() for _ in range(3)]
        OB = [Buf() for _ in range(3)]
        TF = [Buf() for _ in range(2)]
        osl = [p.slot() for _ in range(8)]
        wm2 = ar.mark()
        cnt = [0, 0, 0]

        def evacA(i, tg, bank, PSb):
            tok = slice(t0 + tg * 512, t0 + (tg + 1) * 512)
            if i < 16:
                hl, typ = divmod(i, 8)
                if typ < 4:
                    k = cnt[0] % 3
                    cnt[0] += 1
                    if cnt[0] % 2:
                        p.op("dve", lambda e, k=k, bank=bank: e.tensor_copy(out=of32[k], in_=bank[:, :]), reads=[PSb], writes=[OF[k]])
                    else:
                        p.op("act", lambda e, k=k, bank=bank: e.copy(out=of32[k], in_=bank[:, :]), reads=[PSb], writes=[OF[k]])
                    p.dma("sp", s_qk[hl, typ, :, tok], of32[k], osl[k], reads=[OF[k]])
                elif typ < 6:
                    k = cnt[1] % 3
                    cnt[1] += 1
                    p.op("dve", lambda e, k=k, bank=bank: e.tensor_copy(out=ob16[k], in_=bank[:, :]), reads=[PSb], writes=[OB[k]])
                    p.dma("sp", s_vr[hl, typ - 4, :, tok], ob16[k], osl[3 + k], reads=[OB[k]])
                else:
                    k = cnt[0] % 3
                    cnt[0] += 1
                    p.op("act", lambda e, k=k, bank=bank: e.activation(out=of32[k], in_=bank[:, :], func=AF.Silu), reads=[PSb], writes=[OF[k]])
                    p.dma("sp", s_sgr[hl, typ - 6, :, tok], of32[k], osl[k], reads=[OF[k]])
            elif i < 32:
                fl, typ = divmod(i - 16, 4)
                k = cnt[1] % 3
                cnt[1] += 1
                if typ == 0:
                    p.op("act", lambda e, k=k, bank=bank: e.activation(out=ob16[k], in_=bank[:, :], func=AF.Copy, scale=float(FOX_HD) ** -0.5),
                         reads=[PSb], writes=[OB[k]])
                    dst = s_qf
                elif typ == 1:
                    p.op("dve", lambda e, k=k, bank=bank: e.tensor_copy(out=ob16[k], in_=bank[:, :]), reads=[PSb], writes=[OB[k]])
                    dst = s_kf
                elif typ == 2:
                    p.op("dve", lambda e, k=k, bank=bank: e.tensor_copy(out=ob16[k], in_=bank[:, :]), reads=[PSb], writes=[OB[k]])
                    dst = s_vf
                else:
                    dst = None
                    k = cnt[0] % 3
                    cnt[0] += 1
                    p.op("act", lambda e, k=k, bank=bank: e.activation(out=of32[k], in_=bank[:, :], func=AF.Silu), reads=[PSb], writes=[OF[k]])
                    p.dma("sp", s_sgf[fl, :, tok], of32[k], osl[k], reads=[OF[k]])
                if dst is not None:
                    p.dma("sp", dst[fl, :, tok], ob16[k], osl[3 + k], reads=[OB[k]])
            else:
                k = cnt[2] % 2
                cnt[2] += 1
                p.op("act", lambda e, k=k, bank=bank: e.activation(out=tmpf[k][0:4, :], in_=bank[0:4, :], func=AF.Exp,
                                                                   scale=-1.0, bias=negbf[0:4, 0:1]),
                     reads=[PSb], writes=[TF[k]])
                p.op("act", lambda e, k=k: e.activation(out=tmpf[k][0:4, :], in_=tmpf[k][0:4, :], func=AF.Ln, bias=one_t[0:4, 0:1], scale=1.0),
                     reads=[TF[k]], writes=[TF[k]])
                p.dma("sp", s_lf[:, tok], tmpf[k][0:4, :], osl[6 + k], reads=[TF[k]])

        stage_proj(A_blocks, KC, TS, evacA, wm2)

    import os as _os
    if upto >= 1 and not _os.environ.get('SKIPA'):
        for stile in range(S // TS):
            stage_lnt(x[stile * TS:(stile + 1) * TS, :], TS // 128, TS)
            phaseA_proj(stile * TS)

    def stage_ret():
        ar.reset(base_mark)
        balloc = BigAlloc()
        NTL = S // 512
        intra = ar.alloc(2 * 64, F32, parts=64)
        qdec = [ar.alloc(512, F32) for _ in range(2)]
        kdec = [ar.alloc(512, F32) for _ in range(2)]
        gng = ar.alloc(512, F32, parts=64)
        gnb = ar.alloc(512, F32, parts=64)
        CB = Buf()
        s0 = p.slot()
        for h in range(2):
            p.dma("sp", intra[:, h * 64:(h + 1) * 64], intra_d[h], s0, writes=[CB])
            p.dma("sp", qdec[h], qdec_d[h], s0, writes=[CB])
            p.dma("sp", kdec[h], kdec_d[h], s0, writes=[CB])
        p.dma("sp", gng, rows_d[4 * D:4 * D + 512].partition_broadcast(64), s0, writes=[CB])
        p.dma("sp", gnb, rows_d[4 * D + 512:4 * D + 1024].partition_broadcast(64), s0, writes=[CB])
        raw = [[balloc(512, F32) for _ in range(4)] for _ in range(2)]
        cs = [[balloc(512, F32) for _ in range(2)] for _ in range(2)]
        vT = [[balloc(512, BF16) for _ in range(2)] for _ in range(2)]
        sg = [balloc(1024, F32) for _ in range(2)]
        t1 = [balloc(512, F32) for _ in range(4)]
        rot = [[balloc(512, BF16) for _ in range(8)] for _ in range(2)]
        kv = [balloc(512, BF16) for _ in range(2)]
        ssb = [balloc(64, BF16) for _ in range(2)]
        state = balloc(512, F32)
        stb = [balloc(512, BF16) for _ in range(2)]
        gst = [balloc(6, F32) for _ in range(2)]
        gmv = [balloc(2, F32) for _ in range(2)]
        gsd = [balloc(1, F32) for _ in range(2)]
        grs = [balloc(1, F32) for _ in range(2)]
        ynf = [balloc(256, F32) for _ in range(2)]
        ynb = [balloc(8 * 256, BF16) for _ in range(2)]
        yts = [balloc(1024, BF16) for _ in range(2)]
        yaf = [balloc(512, F32) for _ in range(2)]
        YAF = [Buf() for _ in range(2)]
        RAW = [Buf() for _ in range(2)]
        ROTQ = [Buf() for _ in range(2)]
        ROTK = [Buf() for _ in range(2)]
        T1 = [Buf() for _ in range(4)]
        KV = [Buf() for _ in range(2)]
        SSB = [Buf() for _ in range(2)]
        STATE = Buf()
        STB = [Buf() for _ in range(2)]
        GS = [Buf() for _ in range(2)]
        YNF = [Buf() for _ in range(2)]
        YNB = [Buf() for _ in range(2)]
        YTS = [Buf() for _ in range(2)]
        BT = [Buf(), Buf()]
        BSO = [Buf(), Buf()]
        BU = [Buf(), Buf()]
        BYT = [Buf(), Buf()]
        lsl = [p.slot() for _ in range(2)]
        ysl = [p.slot() for _ in range(2)]

        def load_tile(hl, tl, k):
            tok = slice(tl * 512, (tl + 1) * 512)
            for c in range(4):
                p.dma("sp", raw[k][c], s_qk[hl, c, :, tok], lsl[k], writes=[RAW[k]] if c == 0 else [])
            p.dma("sp", cs[k][0], cos_d[:, tok], lsl[k])
            p.dma("sp", cs[k][1], sin_d[:, tok], lsl[k])
            for ec in range(2):
                p.dma("sp", vT[k][ec], s_vr[hl, ec, :, tok], lsl[k])
                p.dma("sp", sg[k][:, ec * 512:(ec + 1) * 512], s_sgr[hl, ec, :, tok], lsl[k])
            RAW[k].w = {("d", lsl[k].sid): lsl[k].count}
            RAW[k].r = {}

        n = 0
        gt = 0
        for hl in range(2):
            p.op("dve", lambda e: e.memset(state, 0.0), writes=[STATE])
            p.op("dve", lambda e, m=n % 2: e.memset(stb[m], 0.0), writes=[STB[n % 2]])
            load_tile(hl, 0, gt % 2)
            for tl in range(NTL):
                k = gt % 2
                gt += 1
                if tl + 1 < NTL:
                    load_tile(hl, tl + 1, 1 - k)
                R = rot[k]
                for qi, base in ((0, 0), (1, 4)):
                    x1, x2 = raw[k][2 * qi], raw[k][2 * qi + 1]
                    cosv, sinv = cs[k]
                    dec = qdec[hl] if qi == 0 else kdec[hl]
                    eng_a = "dve"
                    ta, tb_ = t1[2 * qi], t1[2 * qi + 1]
                    TA, TBb = T1[2 * qi], T1[2 * qi + 1]
                    p.op(eng_a, lambda e, ta=ta, x1=x1, cosv=cosv: e.tensor_tensor(out=ta, in0=x1, in1=cosv, op=ALU.mult), reads=[RAW[k]], writes=[TA])
                    p.op(eng_a, lambda e, tb_=tb_, x2=x2, sinv=sinv: e.tensor_tensor(out=tb_, in0=x2, in1=sinv, op=ALU.mult), reads=[RAW[k]], writes=[TBb])
                    p.op(eng_a, lambda e, ta=ta, tb_=tb_: e.tensor_tensor(out=ta, in0=ta, in1=tb_, op=ALU.subtract), reads=[TA, TBb], writes=[TA])
                    p.op("act", lambda e, ta=ta, o=R[base]: e.copy(out=o, in_=ta), reads=[TA], writes=[ROTQ[k] if qi == 0 else ROTK[k]])
                    p.op(eng_a, lambda e, ta=ta, o=R[base + 2], dec=dec: e.tensor_tensor(out=o, in0=ta, in1=dec, op=ALU.mult), reads=[TA, CB])
                    p.op(eng_a, lambda e, ta=ta, x1=x1, sinv=sinv: e.tensor_tensor(out=ta, in0=x1, in1=sinv, op=ALU.mult), reads=[RAW[k]], writes=[TA])
                    p.op(eng_a, lambda e, tb_=tb_, x2=x2, cosv=cosv: e.tensor_tensor(out=tb_, in0=x2, in1=cosv, op=ALU.mult), reads=[RAW[k]], writes=[TBb])
                    p.op(eng_a, lambda e, ta=ta, tb_=tb_: e.tensor_tensor(out=ta, in0=ta, in1=tb_, op=ALU.add), reads=[TA, TBb], writes=[TA])
                    p.op("act", lambda e, ta=ta, o=R[base + 1]: e.copy(out=o, in_=ta), reads=[TA])
                    p.op(eng_a, lambda e, ta=ta, o=R[base + 3], dec=dec: e.tensor_tensor(out=o, in0=ta, in1=dec, op=ALU.mult), reads=[TA, CB])
                    if qi == 1:
                        pass
                p.last(ROTQ[k], "dve", "act")
                p.last(ROTK[k], "dve", "act")
                qa, qb, qda, qdb, ka, kb_, kda, kdb = R
                def emit_tr(c, m):
                    cs_ = slice(c * 64, (c + 1) * 64)
                    tv = banks[m][0:64, 0:256].bitcast(BF16)
                    srcs = [kda[:, cs_], kdb[:, cs_], vT[k][0][:, cs_], vT[k][1][:, cs_]]
                    for si, src in enumerate(srcs):
                        p.op("pe", lambda e, tv=tv, si=si, src=src: e.transpose(out=tv[:, si * 128:(si + 1) * 128], in_=src, identity=ident_bf),
                             reads=[ROTQ[k], ROTK[k], RAW[k]], writes=[BT[m]] if si == 0 else [])
                    p.last(BT[m], "pe")
                    p.op("act", lambda e, m=m, tv=tv: e.copy(out=kv[m][0:64, :], in_=tv), reads=[BT[m]], writes=[KV[m]])

                def gn_b(m, c):
                    so = banks[2 + m]
                    p.op("dve", lambda e, m=m: e.reciprocal(out=grs[m][0:64, :], in_=gsd[m][0:64, :]), reads=[GS[m]], writes=[GS[m]])
                    p.op("dve", lambda e, so=so, m=m, k=k, c=c: e.tensor_scalar(out=ynb[k][0:64, c * 256:(c + 1) * 256], in0=so[0:64, 0:256],
                                                                          scalar1=gmv[m][0:64, 0:1], scalar2=grs[m][0:64, 0:1],
                                                                          op0=ALU.subtract, op1=ALU.mult),
                         reads=[BSO[m], GS[m]], writes=[YNB[k]] if c == 0 else [])

                pend_b = None
                emit_tr(0, n % 2)
                for c in range(8):
                    m = n % 2
                    cs_ = slice(c * 64, (c + 1) * 64)
                    if c + 1 < 8:
                        emit_tr(c + 1, 1 - m)
                    ub = banks[4 + m]
                    p.op("pe", lambda e, ub=ub, m=m: e.matmul(ub[:, 0:256], lhsT=kv[m][0:64, 0:128], rhs=kv[m][0:64, 256:512], start=True, stop=True),
                         reads=[KV[m]], writes=[BU[m]])
                    p.op("pe", lambda e, ub=ub, m=m: e.matmul(ub[:, 256:512], lhsT=kv[m][0:64, 128:256], rhs=kv[m][0:64, 256:512], start=True, stop=True),
                         reads=[KV[m]])
                    p.last(BU[m], "pe")
                    p.op("dve", lambda e, ub=ub, hl=hl: e.scalar_tensor_tensor(out=state, in0=state, scalar=cvec[:, CV_CD + hl:CV_CD + hl + 1],
                                                                              in1=ub[:, :], op0=ALU.mult, op1=ALU.add),
                         reads=[BU[m], STATE, cB], writes=[STATE])
                    p.op("act", lambda e, m=m: e.copy(out=stb[1 - m], in_=state), reads=[STATE], writes=[STB[1 - m]])
                    so = banks[2 + m]
                    p.op("pe", lambda e, so=so, cs_=cs_, ka=ka, qa=qa: e.matmul(so[0:64, 256:320], lhsT=ka[:, cs_], rhs=qa[:, cs_], start=True, stop=False),
                         reads=[ROTQ[k], ROTK[k]], writes=[BSO[m]])
                    p.op("pe", lambda e, so=so, cs_=cs_, kb_=kb_, qb=qb: e.matmul(so[0:64, 256:320], lhsT=kb_[:, cs_], rhs=qb[:, cs_], start=False, stop=True),
                         reads=[ROTQ[k], ROTK[k]])
                    p.last(BSO[m], "pe")
                    p.op("dve", lambda e, so=so, m=m, hl=hl: e.tensor_tensor(out=ssb[m][0:64, :], in0=so[0:64, 256:320],
                                                                             in1=intra[:, hl * 64:(hl + 1) * 64], op=ALU.mult),
                         reads=[BSO[m], CB], writes=[SSB[m]])
                    p.op("pe", lambda e, so=so, m=m: e.matmul(so[0:64, 0:256], lhsT=ssb[m][0:64, :], rhs=kv[m][0:64, 256:512], start=True, stop=False),
                         reads=[SSB[m], KV[m]], writes=[BSO[m]])
                    p.op("pe", lambda e, so=so, m=m, cs_=cs_, qda=qda: e.matmul(so[0:64, 0:256], lhsT=qda[:, cs_], rhs=stb[m][:, 0:256], start=False, stop=False),
                         reads=[ROTQ[k], STB[m]])
                    p.op("pe", lambda e, so=so, m=m, cs_=cs_, qdb=qdb: e.matmul(so[0:64, 0:256], lhsT=qdb[:, cs_], rhs=stb[m][:, 256:512], start=False, stop=True),
                         reads=[ROTQ[k], STB[m]])
                    p.last(BSO[m], "pe")
                    if pend_b is not None:
                        gn_b(*pend_b)
                    p.op("dve", lambda e, so=so, m=m: e.bn_stats(out=gst[m][0:64, :], in_=so[0:64, 0:256]), reads=[BSO[m]], writes=[GS[m]])
                    p.op("dve", lambda e, m=m: e.bn_aggr(out=gmv[m][0:64, :], in_=gst[m][0:64, :]), reads=[GS[m]], writes=[GS[m]])
                    p.op("act", lambda e, m=m: e.activation(out=gsd[m][0:64, :], in_=gmv[m][0:64, 1:2], func=AF.Sqrt, bias=eps_t[0:64, 0:1], scale=1.0),
                         reads=[GS[m]], writes=[GS[m]])
                    pend_b = (m, c)
                    n += 1
                gn_b(*pend_b)
                p.last(YNB[k], "dve")
                yb = banks[6 + k][:, :].bitcast(BF16)
                for ec in range(2):
                    for c in range(8):
                        p.op("pe", lambda e, yb=yb, ec=ec, c=c, k=k: e.transpose(out=yb[:, ec * 512 + c * 64: ec * 512 + (c + 1) * 64],
                                                                             in_=ynb[k][0:64, c * 256 + ec * 128: c * 256 + (ec + 1) * 128],
                                                                             identity=ident_bf[0:64, 0:64]),
                             reads=[YNB[k]], writes=[BYT[k]] if (ec == 0 and c == 0) else [])
                p.last(BYT[k], "pe")
                for ec in range(2):
                    gi = CV_GN + hl * 2 + ec
                    p.op("dve", lambda e, yb=yb, ec=ec, gi=gi: e.tensor_scalar(out=yaf[ec], in0=yb[:, ec * 512:(ec + 1) * 512],
                                                                            scalar1=cvec[:, gi:gi + 1], scalar2=cvec[:, gi + 4:gi + 5],
                                                                            op0=ALU.mult, op1=ALU.add),
                         reads=[BYT[k], cB], writes=[YAF[ec]])
                    p.op("dve", lambda e, ec=ec, k=k: e.tensor_tensor(out=yts[k][:, ec * 512:(ec + 1) * 512], in0=yaf[ec],
                                                                      in1=sg[k][:, ec * 512:(ec + 1) * 512], op=ALU.mult),
                         reads=[YAF[ec], RAW[k]], writes=[YTS[k]] if ec == 0 else [])
                p.last(YTS[k], "dve")
                for ec in range(2):
                    p.dma("sp", cin_ap(hl * 2 + ec, tl * 512), yts[k][:, ec * 512:(ec + 1) * 512],
                          ysl[k], reads=[YTS[k]])
        barrier(p)

    if upto >= 2:
        stage_ret()

    def stage_fox():
        ar.reset(base_mark)
        balloc = BigAlloc()
        NQG = S // 512
        triu = ar.alloc(128, F32)
        stri = ar.alloc(NBK, F32, parts=NBK)
        maskT = ar.alloc(4 * 512, BF16)
        lfrow = ar.alloc(128, F32, parts=NBK)
        lfcol = ar.alloc(NBK, F32)
        rsum = ar.alloc(1, F32, parts=NBK)
        rsb = ar.alloc(128, F32, parts=NBK)
        ccol = ar.alloc(NBK, F32)
        crow_f = ar.alloc(128, F32, parts=NBK)
        cr_h = ar.alloc(128, BF16, parts=NBK)
        cr_r = ar.alloc(128, F32, parts=NBK)
        cr_m = ar.alloc(128, BF16, parts=NBK)
        cr_l = ar.alloc(128, BF16, parts=NBK)
        pT = [ar.alloc(512, BF16) for _ in range(3)]
        rl = ar.alloc(512, F32)
        of = ar.alloc(512, F32)
        yo = [ar.alloc(512, BF16) for _ in range(2)]
        qT = balloc(S, BF16)
        kT = balloc(S, BF16)
        vTt = balloc(S, BF16)
        sgT = balloc(S, F32)
        V = balloc(S, BF16)
        crow3 = balloc(S, BF16, parts=3)
        CB = Buf()
        s0 = p.slot()
        p.dma("sp", triu, triu_d, s0, writes=[CB])
        p.dma("sp", stri, stri_d, s0, writes=[CB])
        for r in range(4):
            p.dma("sp", maskT[:, r * 512:(r + 1) * 512], mask_d[r], s0, writes=[CB] if r == 0 else [])
        CB.w = {("d", s0.sid): s0.count}
        HB = Buf()
        LF = Buf()
        CC = Buf()
        CR = Buf()
        CR3 = Buf()
        CRD = Buf()
        VB = Buf()
        PT = [Buf() for _ in range(3)]
        BST = [Buf(), Buf()]
        BO = [Buf(), Buf()]
        BL = [Buf(), Buf()]
        BX = [Buf(), Buf()]
        RL = Buf()
        OFb = Buf()
        YO = [Buf(), Buf()]
        hsl = p.slot()
        lsl = p.slot()
        csl = p.slot()
        c3sl = p.slot()
        ysl = [p.slot() for _ in range(2)]

        for fl in range(4):
            p.dma("sp", qT, s_qf[fl], hsl, writes=[HB])
            p.dma("sp", kT, s_kf[fl], hsl)
            p.dma("sp", vTt, s_vf[fl], hsl)
            p.dma("sp", sgT, s_sgf[fl], hsl)
            HB.w = {("d", hsl.sid): hsl.count}
            p.dma("sp", lfrow, s_lf[fl].rearrange("(n q) -> n q", q=128), lsl, writes=[LF])
            bx = banks[6]
            p.op("pe", lambda e, bx=bx: e.transpose(out=bx[:, 0:NBK], in_=lfrow, identity=ident_f[0:NBK, 0:NBK]), reads=[LF, cB], writes=[BX[0]])
            p.op("dve", lambda e, bx=bx: e.tensor_copy(out=lfcol, in_=bx[:, 0:NBK]), reads=[BX[0]], writes=[CC])
            p.op("dve", lambda e: e.reduce_sum(out=rsum, in_=lfrow, axis=AX.X), reads=[LF], writes=[CC])
            p.op("dve", lambda e: e.tensor_scalar(out=rsb, in0=ones_f[0:NBK, :], scalar1=rsum[:, 0:1], scalar2=None, op0=ALU.mult),
                 reads=[CC], writes=[CC])
            bx1 = banks[7]
            p.op("pe", lambda e, bx1=bx1: e.matmul(bx1[:, 0:NBK], lhsT=triu, rhs=lfcol, start=True, stop=False), reads=[CC, CB], writes=[BX[1]])
            p.op("pe", lambda e, bx1=bx1: e.matmul(bx1[:, 0:NBK], lhsT=rsb, rhs=stri, start=False, stop=True), reads=[CC, CB])
            p.last(BX[1], "pe")
            p.op("dve", lambda e, bx1=bx1: e.tensor_copy(out=ccol, in_=bx1[:, 0:NBK]), reads=[BX[1]], writes=[CC])
            p.op("pe", lambda e, bx=bx: e.transpose(out=bx[0:NBK, 128:256], in_=ccol, identity=ident_f), reads=[CC], writes=[BX[0]])
            p.op("dve", lambda e, bx=bx: e.tensor_scalar(out=crow_f, in0=bx[0:NBK, 128:256], scalar1=-1.0, scalar2=None, op0=ALU.mult),
                 reads=[BX[0]], writes=[CR])
            p.op("dve", lambda e: e.tensor_copy(out=cr_h, in_=crow_f), reads=[CR], writes=[CR])
            p.op("dve", lambda e: e.tensor_tensor(out=cr_r, in0=crow_f, in1=cr_h, op=ALU.subtract), reads=[CR], writes=[CR])
            p.op("dve", lambda e: e.tensor_copy(out=cr_m, in_=cr_r), reads=[CR], writes=[CR])
            p.op("dve", lambda e: e.tensor_tensor(out=cr_r, in0=cr_r, in1=cr_m, op=ALU.subtract), reads=[CR], writes=[CR])
            p.op("dve", lambda e: e.tensor_copy(out=cr_l, in_=cr_r), reads=[CR], writes=[CR])
            for pi, src in enumerate((cr_h, cr_m, cr_l)):
                p.dma("sp", s_crow[fl, pi].rearrange("(n q) -> n q", q=128), src, csl, reads=[CR], writes=[CRD] if pi == 0 else [])
            CRD.w = {("d", csl.sid): csl.count}
            p.dma("sp", crow3, s_crow[fl], c3sl, reads=[CRD], writes=[CR3])
            ng8 = (NBK + 7) // 8
            for n8 in range(ng8):
                bb = banks[6 + (n8 % 2)][:, :].bitcast(BF16)
                nb8 = min(8, NBK - n8 * 8)
                for j in range(nb8):
                    blk = n8 * 8 + j
                    p.op("pe", lambda e, bb=bb, j=j, blk=blk: e.transpose(out=bb[:, j * 128:(j + 1) * 128],
                                                                      in_=vTt[:, blk * 128:(blk + 1) * 128], identity=ident_bf),
                         reads=[HB], writes=[BX[n8 % 2]] if j == 0 else [])
                p.last(BX[n8 % 2], "pe")
                if n8 % 2 == 0:
                    p.op("act", lambda e, bb=bb, n8=n8, nb8=nb8: e.copy(out=V[:, n8 * 1024:n8 * 1024 + nb8 * 128], in_=bb[:, 0:nb8 * 128]),
                         reads=[BX[n8 % 2]], writes=[VB] if n8 == 0 else [])
                else:
                    p.op("dve", lambda e, bb=bb, n8=n8, nb8=nb8: e.tensor_copy(out=V[:, n8 * 1024:n8 * 1024 + nb8 * 128], in_=bb[:, 0:nb8 * 128]),
                         reads=[BX[n8 % 2]])
            p.last(VB, "act", "dve")
            units = [(qg, kb) for qg in range(NQG) for kb in range(4 * qg + 4)]

            def qk(u):
                qg, kb = units[u]
                bs = banks[u % 2]
                diag = kb >= 4 * qg
                p.op("pe", lambda e, bs=bs, qg=qg, kb=kb: e.matmul(bs[:, :], lhsT=kT[:, kb * 128:(kb + 1) * 128],
                                                               rhs=qT[:, qg * 512:(qg + 1) * 512], start=True, stop=False),
                     reads=[HB], writes=[BST[u % 2]])
                p.op("pe", lambda e, bs=bs, qg=qg, diag=diag: e.matmul(bs[:, :], lhsT=ones_bf[0:3, :], rhs=crow3[:, qg * 512:(qg + 1) * 512],
                                                                      start=False, stop=not diag),
                     reads=[CR3])
                if diag:
                    r = kb - 4 * qg
                    p.op("pe", lambda e, bs=bs, r=r: e.matmul(bs[:, :], lhsT=ident_bf, rhs=maskT[:, r * 512:(r + 1) * 512], start=False, stop=True),
                         reads=[CB])
                p.last(BST[u % 2], "pe")

            qk(0)
            for u, (qg, kb) in enumerate(units):
                if u + 1 < len(units):
                    qk(u + 1)
                bs = banks[u % 2]
                pk = u % 3
                p.op("act", lambda e, bs=bs, pk=pk, kb=kb: e.activation(out=pT[pk], in_=bs[:, :], func=AF.Exp, bias=ccol[:, kb:kb + 1], scale=1.0),
                     reads=[BST[u % 2], CC], writes=[PT[pk]])
                last = kb == 4 * qg + 3
                g2 = qg % 2
                p.op("pe", lambda e, g2=g2, kb=kb, pk=pk, last=last: e.matmul(banks[2 + g2][:, :], lhsT=V[:, kb * 128:(kb + 1) * 128], rhs=pT[pk],
                                                                         start=(kb == 0), stop=last),
                     reads=[PT[pk], VB], writes=[BO[g2]] if kb == 0 else [])
                p.op("pe", lambda e, g2=g2, kb=kb, pk=pk, last=last: e.matmul(banks[4 + g2][:, :], lhsT=ones_bf, rhs=pT[pk],
                                                                         start=(kb == 0), stop=last),
                     reads=[PT[pk]], writes=[BL[g2]] if kb == 0 else [])
                if last:
                    p.last(BO[g2], "pe")
                    p.last(BL[g2], "pe")
                    p.op("dve", lambda e, g2=g2: e.reciprocal(out=rl, in_=banks[4 + g2][:, :]), reads=[BL[g2]], writes=[RL])
                    p.op("dve", lambda e, g2=g2: e.tensor_tensor(out=of, in0=banks[2 + g2][:, :], in1=rl, op=ALU.mult), reads=[BO[g2], RL], writes=[OFb])
                    yk = qg % 2
                    p.op("dve", lambda e, yk=yk, qg=qg: e.tensor_tensor(out=yo[yk], in0=of, in1=sgT[:, qg * 512:(qg + 1) * 512], op=ALU.mult),
                         reads=[OFb, HB], writes=[YO[yk]])
                    p.dma("sp", cin_ap(4 + fl, qg * 512), yo[yk], ysl[yk], reads=[YO[yk]])
        barrier(p)

    if upto >= 3:
        stage_fox()

    XC = Buf()
    xsl = p.slot()
    p.pinned = {xsl.sid}
    if upto >= 3 and dbg:
        dsl = p.slot()
        p.dma("sp", cin_dbg, cin, dsl)
    if upto >= 4:
        for cq in range(4):
            for rb in range(8):
                p.custom("pool", lambda e, cq=cq, rb=rb: e.collective_compute("AllGather", ALU.bypass,
                                                                             replica_groups=[[0, 1, 2, 3], [4, 5, 6, 7]],
                                                                             ins=[cin[cq, rb * 128:(rb + 1) * 128, :]],
                                                                             outs=[cout[cq, rb * 512:(rb + 1) * 512, :]]), xsl, 1, writes=[XC])

    def sp_setup(e):
        reg = st.enter_context(e.register("tokoff"))
        e.reg_load(reg, tokoff_d[0:1, 0:1])
        p.env["tokoff"] = e.snap(reg, min_val=0, max_val=3)
    if upto >= 5:
        p.pre["sp"].append(sp_setup)

    def phaseB():
        stage_lnt(xoff, TB // 128, TB)
        ar.reset(base_mark)
        gof = [ar.alloc(512, F32) for _ in range(3)]
        GO = [Buf() for _ in range(3)]
        gsl = [p.slot() for _ in range(3)]
        wmB = ar.mark()
        gcnt = [0]

        def evacG(i, tg, bank, PSb):
            k = gcnt[0] % 3
            gcnt[0] += 1
            p.op("act", lambda e, k=k, bank=bank, i=i: e.activation(out=gof[k], in_=bank[:, :], func=AF.Sigmoid,
                                                                  bias=cvec[:, CV_BG + i:CV_BG + i + 1], scale=1.0),
                 reads=[PSb], writes=[GO[k]])
            p.dma("sp", s_gates[i, :, tg * 512:(tg + 1) * 512], gof[k], gsl[k], reads=[GO[k]])

        stage_proj([(wg[cb], 128, KC, list(range(KC))) for cb in range(NCG)], KC, TB, evacG, wmB)

        ar.reset(base_mark)
        aT = actT_view(32, TB)
        ysl2 = p.slot()
        p.wait_all("sp", [XC])
        p.pinned = set()
        tvv = actT_raw[:, 0:32 * TB].rearrange("p (r b t) -> p r b t", r=4, b=8)
        for rb in range(8):
            def fn(e, rb=rb):
                src = cout[bass.ds(p.env["tokoff"], 1), rb * 512:(rb + 1) * 512, :].rearrange("o (r p) t -> p (o r) t", p=128)
                return e.dma_start(out=tvv[:, :, rb, :], in_=src)
            ysl2.count += 16
            p.q["sp"].append([fn, {}, False, ysl2.sid])
        barrier(p)
        NG = TB // 512
        gbuf = [[ar.alloc(512, F32) for _ in range(NG)] for _ in range(2)]
        tmpb = [ar.alloc(512, F32) for _ in range(NG)]
        mf = [ar.alloc(512, F32) for _ in range(2)]
        mb = [ar.alloc(512, BF16) for _ in range(2)]
        GB = [[Buf() for _ in range(NG)] for _ in range(2)]
        TMP = [Buf() for _ in range(NG)]
        MF = [Buf() for _ in range(2)]
        MB = [Buf() for _ in range(2)]
        g2sl = [p.slot() for _ in range(2)]
        msl = [p.slot() for _ in range(2)]
        wm3 = ar.mark()
        blocks = []
        for cb in range(NCD):
            blocks.append((wbr[cb], 128, 16, [(i // 4) * 8 + (i % 4) for i in range(16)]))
            blocks.append((wbf[cb], 128, 16, [(i // 4) * 8 + 4 + (i % 4) for i in range(16)]))
        mcnt = [0]

        def preM(i):
            cb, br = divmod(i, 2)
            k = i % 2
            for tg in range(NG):
                p.dma("sp", gbuf[k][tg], s_gates[br * NCD + cb, :, tg * 512:(tg + 1) * 512], g2sl[k], writes=[GB[k][tg]])
            for tg in range(NG):
                GB[k][tg].w = {("d", g2sl[k].sid): g2sl[k].count}

        def evacM(i, tg, bank, PSb):
            cb, br = divmod(i, 2)
            k = i % 2
            if br == 0:
                p.op("dve", lambda e, tg=tg, k=k, bank=bank: e.tensor_tensor(out=tmpb[tg], in0=bank[:, :], in1=gbuf[k][tg], op=ALU.mult),
                     reads=[PSb, GB[k][tg]], writes=[TMP[tg]])
            else:
                j = mcnt[0] % 2
                mcnt[0] += 1
                p.op("dve", lambda e, tg=tg, k=k, bank=bank, j=j: e.tensor_tensor(out=mf[j], in0=bank[:, :], in1=gbuf[k][tg], op=ALU.mult),
                     reads=[PSb, GB[k][tg]], writes=[MF[j]])
                p.op("dve", lambda e, tg=tg, j=j: e.tensor_tensor(out=mb[j], in0=mf[j], in1=tmpb[tg], op=ALU.add),
                     reads=[MF[j], TMP[tg]], writes=[MB[j]])
                p.dma("sp", s_merged[cb, :, tg * 512:(tg + 1) * 512], mb[j], msl[j], reads=[MB[j]])

        stage_proj(blocks, 32, TB, evacM, wm3, pre=preM)

        ar.reset(base_mark)
        aT = actT_view(KC, TB)
        msl2 = p.slot()
        for c in range(KC):
            p.dma("sp", aT[:, c, :], s_merged[c], msl2)
        barrier(p)
        yoT = [ar.alloc(512, F32) for _ in range(2)]
        yst = [ar.alloc(512, F32) for _ in range(2)]
        YOT = [Buf() for _ in range(2)]
        YST = [Buf() for _ in range(2)]
        BTR = [Buf() for _ in range(2)]
        osl = [p.slot() for _ in range(2)]
        wm4 = ar.mark()
        ocnt = [0]

        def evacO(i, tg, bank, PSb):
            j = ocnt[0] % 2
            ocnt[0] += 1
            p.op("act", lambda e, j=j, bank=bank: e.copy(out=yoT[j], in_=bank[:, :]), reads=[PSb], writes=[YOT[j]])
            tb = banks[4 + j]
            for q in range(4):
                p.op("pe", lambda e, tb=tb, j=j, q=q: e.transpose(out=tb[:, q * 128:(q + 1) * 128], in_=yoT[j][:, q * 128:(q + 1) * 128], identity=ident_f),
                     reads=[YOT[j]], writes=[BTR[j]] if q == 0 else [])
            p.last(BTR[j], "pe")
            p.op("dve", lambda e, tb=tb, j=j: e.tensor_copy(out=yst[j], in_=tb[:, :]), reads=[BTR[j]], writes=[YST[j]])
            p.dma("sp", s_yo[tg * 512:(tg + 1) * 512, i * 128:(i + 1) * 128].rearrange("(q p) f -> p q f", p=128),
                  yst[j].rearrange("p (q f) -> p q f", q=4), osl[j], reads=[YST[j]])

        stage_proj([(wo[cb], 128, KC, list(range(KC))) for cb in range(NCD)], KC, TB, evacO, wm4)

        ar.reset(base_mark)
        balloc = BigAlloc()
        gin = balloc(D, F32)
        bin_ = balloc(D, F32)
        gpo = balloc(D, F32)
        bpo = balloc(D, F32)
        RB = Buf()
        rsl = p.slot()
        for i, t in enumerate((gin, bin_, gpo, bpo)):
            p.dma("sp", t, rows_d[i * D:(i + 1) * D].partition_broadcast(128), rsl, writes=[RB] if i == 0 else [])
        RB.w = {("d", rsl.sid): rsl.count}
        xb = [ar.alloc(D, F32) for _ in range(2)]
        yb = [ar.alloc(D, F32) for _ in range(2)]
        nch = max(D // 512, 1)
        cw = D // nch
        stt = [ar.alloc(nch * 6, F32) for _ in range(2)]
        mv = [ar.alloc(2, F32) for _ in range(2)]
        sd = [ar.alloc(1, F32) for _ in range(2)]
        rs = [ar.alloc(1, F32) for _ in range(2)]
        XB = [Buf() for _ in range(2)]
        YB = [Buf() for _ in range(2)]
        SM = [Buf() for _ in range(2)]
        xsl_ = [p.slot() for _ in range(2)]
        ysl_ = [p.slot() for _ in range(2)]
        osl2 = [p.slot() for _ in range(2)]
        ntt = TB // 128

        def loadB4(tt, k):
            p.dma("sp", xb[k], xoff[tt * 128:(tt + 1) * 128, :], xsl_[k], writes=[XB[k]])
            p.dma("sp", yb[k], s_yo[tt * 128:(tt + 1) * 128, :], ysl_[k], writes=[YB[k]])

        def lnstats(k, src, SRCB):
            for i in range(nch):
                p.op("dve", lambda e, k=k, i=i, src=src: e.bn_stats(out=stt[k][:, i * 6:(i + 1) * 6], in_=src[:, i * cw:(i + 1) * cw]),
                     reads=[SRCB], writes=[SM[k]] if i == 0 else [])
            p.op("dve", lambda e, k=k: e.bn_aggr(out=mv[k], in_=stt[k]), reads=[SM[k]], writes=[SM[k]])
            p.op("act", lambda e, k=k: e.activation(out=sd[k], in_=mv[k][:, 1:2], func=AF.Sqrt, bias=eps_t[:, 0:1], scale=1.0),
                 reads=[SM[k]], writes=[SM[k]])
            p.op("dve", lambda e, k=k: e.reciprocal(out=rs[k], in_=sd[k]), reads=[SM[k]], writes=[SM[k]])

        nm = [ar.alloc(1, F32) for _ in range(2)]

        def norm_act(k, buf, BUFB):
            p.op("dve", lambda e, k=k: e.tensor_scalar(out=nm[k], in0=mv[k][:, 0:1], scalar1=rs[k][:, 0:1], scalar2=-1.0,
                                                       op0=ALU.mult, op1=ALU.mult), reads=[SM[k]], writes=[SM[k]])
            p.op("act", lambda e, k=k, buf=buf: e.activation(out=buf, in_=buf, func=AF.Identity, scale=rs[k][:, 0:1], bias=nm[k][:, 0:1]),
                 reads=[BUFB, SM[k]], writes=[BUFB])

        loadB4(0, 0)
        for tt in range(ntt):
            k = tt % 2
            if tt + 1 < ntt:
                loadB4(tt + 1, 1 - k)
            lnstats(k, xb[k], XB[k])
            norm_act(k, xb[k], XB[k])
            p.op("dve", lambda e, k=k: e.tensor_tensor(out=xb[k], in0=xb[k], in1=gin, op=ALU.mult), reads=[XB[k], RB], writes=[XB[k]])
            p.op("dve", lambda e, k=k: e.tensor_tensor(out=xb[k], in0=xb[k], in1=bin_, op=ALU.add), reads=[XB[k], RB], writes=[XB[k]])
            p.op("dve", lambda e, k=k: e.scalar_tensor_tensor(out=yb[k], in0=xb[k], scalar=float(cfg.alpha), in1=yb[k],
                                                              op0=ALU.mult, op1=ALU.add), reads=[XB[k], YB[k]], writes=[YB[k]])
            lnstats(k, yb[k], YB[k])
            norm_act(k, yb[k], YB[k])
            p.op("dve", lambda e, k=k: e.tensor_tensor(out=yb[k], in0=yb[k], in1=gpo, op=ALU.mult), reads=[YB[k], RB], writes=[YB[k]])
            p.op("dve", lambda e, k=k: e.tensor_tensor(out=yb[k], in0=yb[k], in1=bpo, op=ALU.add), reads=[YB[k], RB], writes=[YB[k]])
            p.dma("sp", out[tt * 128:(tt + 1) * 128, :], yb[k], osl2[k], reads=[YB[k]])
        barrier(p)

    if upto >= 5:
        phaseB()
    barrier(p)
    p.replay(st)
    return nc, st, p

import ml_dtypes
from concourse.bass_utils import run_bass_kernel_spmd

_PROG_CACHE = {}


def _tile_w(wblk, M):
    K = wblk.shape[0]
    return np.ascontiguousarray(wblk.reshape(K // 128, 128, M).transpose(1, 0, 2).reshape(128, (K // 128) * M))


def _consts(cfg):
    S, NBK = cfg.S, cfg.NBK
    half = RET_HD // 2
    inv_freq = (10000.0 ** (-np.arange(half, dtype=np.float32) / np.float32(half))).astype(np.float32)
    pos = np.arange(S, dtype=np.float32)
    ang = (pos[:, None] * inv_freq[None, :]).astype(np.float32)
    c = {}
    c["cos"] = np.ascontiguousarray(np.cos(ang.astype(np.float64)).astype(np.float32).T)
    c["sin"] = np.ascontiguousarray(np.sin(ang.astype(np.float64)).astype(np.float32).T)
    c["ident_bf"] = np.eye(128, dtype=np.float32).astype(ml_dtypes.bfloat16)
    c["ident_f"] = np.eye(128, dtype=np.float32)
    c["triu"] = np.triu(np.ones((128, 128), np.float32))
    c["stri"] = np.triu(np.ones((NBK, NBK), np.float32), k=1)
    kk = np.arange(128)[None, :, None] + 128 * np.arange(4)[:, None, None]
    qq = np.arange(512)[None, None, :]
    c["maskT"] = np.where(kk > qq, -30000.0, 0.0).astype(np.float32).astype(ml_dtypes.bfloat16)
    return c


def _ret_consts(heads):
    out = {}
    idx = np.arange(64, dtype=np.float64)
    intra, qd, kd, cd = [], [], [], []
    for h in heads:
        lg = np.log1p(-(2.0 ** (-5.0 - h)))
        sc = RET_HD ** -0.5
        intra.append(np.exp(lg * np.abs(idx[:, None] - idx[None, :])) * sc)
        qrow = np.exp(lg * idx)
        krow = np.exp(lg * (64 - idx)) * sc
        qd.append(np.broadcast_to(np.tile(qrow, 8)[None, :], (128, 512)))
        kd.append(np.broadcast_to(np.tile(krow, 8)[None, :], (128, 512)))
        cd.append(np.exp(lg * 64))
    out["intra"] = np.ascontiguousarray(np.stack(intra).astype(np.float32))
    out["qdec"] = np.ascontiguousarray(np.stack(qd).astype(np.float32))
    out["kdec"] = np.ascontiguousarray(np.stack(kd).astype(np.float32))
    return out, cd


def make_in_maps(cfg, x, ln_in_g, ln_in_b, w_in, b_gate, b_forget, ret_norm_g, ret_norm_b,
                 w_branch_ret, w_branch_fox, w_out, ln_post_g, ln_post_b):
    D, S, KC, TB = cfg.D, cfg.S, cfg.KC, cfg.TB
    NCG, NCD = 2 * D // 128, D // 128
    f32 = np.float32
    x = np.asarray(x, f32)
    w_in = np.asarray(w_in, f32)[0]
    RW, FW = 2048, 2048
    o_qr, o_kr, o_vr, o_gr = 0, RW, 2 * RW, 3 * RW
    o_qf, o_kf, o_vf, o_gf = 4 * RW, 4 * RW + FW, 4 * RW + 2 * FW, 4 * RW + 3 * FW
    o_fl = 4 * RW + 4 * FW
    o_gate = o_fl + 16
    consts = _consts(cfg)
    wg = np.stack([_tile_w(w_in[:, o_gate + cb * 128:o_gate + (cb + 1) * 128], 128) for cb in range(NCG)])
    wbr_full = np.asarray(w_branch_ret, f32)[0]
    wbf_full = np.asarray(w_branch_fox, f32)[0]
    wo_full = np.asarray(w_out, f32)[0]
    wbr = np.stack([_tile_w(wbr_full[:, cb * 128:(cb + 1) * 128], 128) for cb in range(NCD)])
    wbf = np.stack([_tile_w(wbf_full[:, cb * 128:(cb + 1) * 128], 128) for cb in range(NCD)])
    wo = np.stack([_tile_w(wo_full[:, cb * 128:(cb + 1) * 128], 128) for cb in range(NCD)])
    g_in, b_in = np.asarray(ln_in_g, f32), np.asarray(ln_in_b, f32)
    g_po, b_po = np.asarray(ln_post_g, f32)[0], np.asarray(ln_post_b, f32)[0]
    bg = np.asarray(b_gate, f32)[0]
    bfg = np.asarray(b_forget, f32)[0]
    rg, rb = np.asarray(ret_norm_g, f32)[0], np.asarray(ret_norm_b, f32)[0]
    in_maps = []
    for core in range(8):
        b, g = divmod(core, 4)
        m = dict(consts)
        m["x"] = np.ascontiguousarray(x[b])
        m["xoff_rows"] = np.ascontiguousarray(x[b, g * TB:(g + 1) * TB])
        m["tokoff"] = np.array([[g]], np.int32)
        blocks = []
        for hl in range(2):
            hh = 2 * g + hl
            for off in (o_qr, o_kr, o_vr, o_gr):
                for ec in range(2):
                    c0 = off + hh * 256 + ec * 128
                    blocks.append(_tile_w(w_in[:, c0:c0 + 128], 128))
        for fl in range(4):
            fh = 4 * g + fl
            for off in (o_qf, o_kf, o_vf, o_gf):
                c0 = off + fh * 128
                blocks.append(_tile_w(w_in[:, c0:c0 + 128], 128))
        m["wa"] = np.stack(blocks)
        m["wf"] = _tile_w(w_in[:, o_fl + 4 * g:o_fl + 4 * g + 4], 4)
        m["wg"], m["wbr"], m["wbf"], m["wo"] = wg, wbr, wbf, wo
        rc, cd = _ret_consts([2 * g, 2 * g + 1])
        m.update(rc)
        NV = 2 * KC + NCG + 2 + 1 + 8
        cv = np.zeros((128, NV), f32)
        cv[:, 0:KC] = g_in.reshape(KC, 128).T
        cv[:, KC:2 * KC] = b_in.reshape(KC, 128).T
        cv[:, 2 * KC:2 * KC + NCG] = bg.reshape(NCG, 128).T
        cv[:, 2 * KC + NCG] = cd[0]
        cv[:, 2 * KC + NCG + 1] = cd[1]
        cv[0:4, 2 * KC + NCG + 2] = bfg[4 * g:4 * g + 4]
        for hl in range(2):
            for ec in range(2):
                f0 = (2 * g + hl) * 256 + ec * 128
                cv[:, 2 * KC + NCG + 3 + hl * 2 + ec] = rg[f0:f0 + 128]
                cv[:, 2 * KC + NCG + 3 + 4 + hl * 2 + ec] = rb[f0:f0 + 128]
        m["cvec"] = cv
        m["rows"] = np.concatenate([g_in, b_in, g_po, b_po, rg[2 * g * 256:(2 * g + 2) * 256], rb[2 * g * 256:(2 * g + 2) * 256]]).astype(f32)
        in_maps.append(m)
    return in_maps


def run(cfg, inputs, dbg=False, upto=99, trace=False):
    key = (cfg.D, cfg.S, cfg.TS, dbg, upto)
    if key not in _PROG_CACHE:
        nc, st, p = build_program(cfg, dbg=dbg, upto=upto)
        _PROG_CACHE[key] = (nc, st, p)
    nc, st, p = _PROG_CACHE[key]
    in_maps = make_in_maps(cfg, **inputs)
    res = run_bass_kernel_spmd(nc, in_maps, core_ids=list(range(8)), trace=trace)
    out = np.zeros((2, cfg.S, cfg.D), np.float32)
    for core in range(8):
        b, g = divmod(core, 4)
        out[b, g * cfg.TB:(g + 1) * cfg.TB] = res.results[core]["out"]
    return out, res


def kernel(**inputs):
    cfg = Cfg(D=4096, S=8192, TS=2048)
    out, _ = run(cfg, inputs)
    return out
```
